# Optimizing a Trainium2 kernel written in Bass

```python
import math
import jax
import jax.numpy as jnp
from jax import lax
import numpy as np

D_MODEL = 1024
BATCH = 8
SEQ = 2048
DEPTH = 2

CONV_A_GROUPS = 4
CONV_A_GROUP_DIM = 64
D_CONV_A = CONV_A_GROUPS * CONV_A_GROUP_DIM
CONV_A_WIDTH = 3
SSD_HEADS = 6
SSD_HEAD_DIM = 64
D_SSD = SSD_HEADS * SSD_HEAD_DIM
SSD_GROUPS = 2
SSD_STATE = 128
SSD_CONV_WIDTH = 4
SSD_CHUNK = 128
SSD_CONV_DIM = D_SSD + 2 * SSD_GROUPS * SSD_STATE
SSD_NORM_EPS = 1e-5
MLA_HEADS = 6
Q_LORA = 256
KV_LORA = 128
QK_NOPE = 64
QK_ROPE = 32
V_DIM = 64
D_MLA = MLA_HEADS * V_DIM
ROPE_BASE = 10000.0
Q_BLOCK = 128
D_MIX = D_CONV_A + D_SSD + D_MLA
NORM_EPS = 1e-6
POS_OFFSET_MAX = 1024
SPLIT_SIZES = (D_CONV_A, D_CONV_A, D_CONV_A, D_CONV_A,
               D_SSD, D_SSD, SSD_GROUPS * SSD_STATE, SSD_GROUPS * SSD_STATE, SSD_HEADS,
               Q_LORA, KV_LORA, QK_ROPE, D_MLA)
IN_COLS = sum(SPLIT_SIZES)

kernel_name = 'hybrid_conv_ssd_mla_parallel'


def rmsnorm(x, g, eps=NORM_EPS):
    xf = x.astype(jnp.float32)
    y = xf * lax.rsqrt(jnp.mean(xf * xf, axis=-1, keepdims=True) + eps)
    return (y * g.astype(jnp.float32)).astype(x.dtype)


def causal_depthwise_conv(u, w):
    k, c = w.shape
    return lax.conv_general_dilated(
        u, w[:, None, :].astype(u.dtype), window_strides=(1,), padding=[(k - 1, 0)],
        dimension_numbers=('NWC', 'WIO', 'NWC'), feature_group_count=c)


def apply_rope(t, cos, sin):
    tf = t.astype(jnp.float32)
    t1, t2 = jnp.split(tf, 2, axis=-1)
    return jnp.concatenate([t1 * cos - t2 * sin, t2 * cos + t1 * sin], axis=-1).astype(t.dtype)


def short_conv_branch(a_h, a_b, a_c, a_z, conv_w):
    return a_b * causal_depthwise_conv(a_c * a_h, conv_w) * jax.nn.silu(a_z)


def segsum_exp(a_cs):
    l = a_cs.shape[-1]
    diff = a_cs[..., :, None] - a_cs[..., None, :]
    mask = jnp.tril(jnp.ones((l, l), dtype=bool))
    return jnp.exp(jnp.where(mask, diff, -jnp.inf))


def ssd_chunked(xh, dt, a, bh, ch):
    b, s, h, p = xh.shape
    n = bh.shape[-1]
    nc = s // SSD_CHUNK
    la = (dt * a).reshape(b, nc, SSD_CHUNK, h).transpose(0, 3, 1, 2)
    xd = (xh * dt[..., None]).reshape(b, nc, SSD_CHUNK, h, p)
    bc = bh.reshape(b, nc, SSD_CHUNK, h, n)
    cc = ch.reshape(b, nc, SSD_CHUNK, h, n)
    a_cs = jnp.cumsum(la, axis=-1)
    scores = jnp.einsum('bclhn,bcshn->bhcls', cc, bc) * segsum_exp(a_cs)
    y_diag = jnp.einsum('bhcls,bcshp->bclhp', scores, xd)
    decay_states = jnp.exp(a_cs[..., -1:] - a_cs)
    states = jnp.einsum('bclhn,bhcl,bclhp->bchpn', bc, decay_states, xd)
    chunk_decay = jnp.exp(a_cs[..., -1])

    def step(carry, inp):
        st, dec = inp
        return carry * dec[..., None, None] + st, carry

    init = jnp.zeros((b, h, p, n), dtype=xd.dtype)
    _, prev = lax.scan(step, init, (jnp.moveaxis(states, 1, 0), jnp.moveaxis(chunk_decay, 2, 0)))
    prev = jnp.moveaxis(prev, 0, 1)
    y_off = jnp.einsum('bclhn,bchpn,bhcl->bclhp', cc, prev, jnp.exp(a_cs))
    return (y_diag + y_off).reshape(b, s, h, p)


def ssd_branch(s_z, s_x, s_b, s_c, s_dt, conv_w, conv_b, dt_bias, a_log, d_skip, norm_g):
    b, s, _ = s_x.shape
    f32 = jnp.float32
    xbc = jnp.concatenate([s_x, s_b, s_c], axis=-1)
    xbc = jax.nn.silu(causal_depthwise_conv(xbc, conv_w) + conv_b)
    xs, bs, cs = jnp.split(xbc, [D_SSD, D_SSD + SSD_GROUPS * SSD_STATE], axis=-1)
    rep = SSD_HEADS // SSD_GROUPS
    xh = xs.reshape(b, s, SSD_HEADS, SSD_HEAD_DIM).astype(f32)
    bh = jnp.repeat(bs.reshape(b, s, SSD_GROUPS, SSD_STATE), rep, axis=2).astype(f32)
    ch = jnp.repeat(cs.reshape(b, s, SSD_GROUPS, SSD_STATE), rep, axis=2).astype(f32)
    dt = jax.nn.softplus(s_dt.astype(f32) + dt_bias.astype(f32))
    a = -jnp.exp(a_log.astype(f32))
    y = ssd_chunked(xh, dt, a, bh, ch) + xh * d_skip.astype(f32)[:, None]
    y = y.reshape(b, s, D_SSD).astype(s_x.dtype)
    g = (y * jax.nn.silu(s_z)).reshape(b, s, SSD_GROUPS, D_SSD // SSD_GROUPS)
    g = rmsnorm(g, norm_g.reshape(SSD_GROUPS, D_SSD // SSD_GROUPS), SSD_NORM_EPS)
    return g.reshape(b, s, D_SSD)


def causal_block_attention(q_nope, q_rope, k_nope, k_rope, v):
    b, s, h, _ = q_nope.shape
    nb = s // Q_BLOCK
    scale = (QK_NOPE + QK_ROPE) ** -0.5
    kpos = jnp.arange(s)

    def to_blocks(t):
        return jnp.swapaxes(t.reshape(b, nb, Q_BLOCK, *t.shape[2:]), 0, 1)

    def one_block(args):
        qn, qr, start = args
        sc = (jnp.einsum('bqhd,bkhd->bhqk', qn, k_nope)
              + jnp.einsum('bqhr,bkr->bhqk', qr, k_rope)).astype(jnp.float32) * scale
        qpos = start + jnp.arange(Q_BLOCK)
        mask = kpos[None, :] <= qpos[:, None]
        pr = jax.nn.softmax(jnp.where(mask, sc, -jnp.inf), axis=-1).astype(v.dtype)
        return jnp.einsum('bhqk,bkhd->bqhd', pr, v)

    out = lax.map(one_block, (to_blocks(q_nope), to_blocks(q_rope), jnp.arange(nb) * Q_BLOCK))
    return jnp.swapaxes(out, 0, 1).reshape(b, s, h, v.shape[-1])


def mla_branch(c_qa, c_kv, c_kr, c_z, cos, sin, q_norm_g, w_qb, kv_norm_g, w_kvb):
    b, s, _ = c_qa.shape
    q = jnp.einsum('bsr,rc->bsc', rmsnorm(c_qa, q_norm_g), w_qb).reshape(b, s, MLA_HEADS, QK_NOPE + QK_ROPE)
    q_nope, q_rope = jnp.split(q, [QK_NOPE], axis=-1)
    q_rope = apply_rope(q_rope, cos[:, :, None, :], sin[:, :, None, :])
    kv = jnp.einsum('bsr,rc->bsc', rmsnorm(c_kv, kv_norm_g), w_kvb).reshape(b, s, MLA_HEADS, QK_NOPE + V_DIM)
    k_nope, v = jnp.split(kv, [QK_NOPE], axis=-1)
    k_rope = apply_rope(c_kr, cos, sin)
    o = causal_block_attention(q_nope, q_rope, k_nope, k_rope, v)
    return o.reshape(b, s, D_MLA) * jax.nn.silu(c_z)


def hybrid_layer(x, cos, sin, norm_g, w_in, conv_a_w, ssd_conv_w, ssd_conv_b, ssd_dt_bias,
                 ssd_a_log, ssd_d, ssd_norm_g, mla_q_norm_g, w_qb, mla_kv_norm_g, w_kvb, w_out):
    h = rmsnorm(x, norm_g)
    proj = jnp.einsum('bsd,dc->bsc', h, w_in)
    split_at = np.cumsum(SPLIT_SIZES)[:-1].tolist()
    (a_h, a_b, a_c, a_z, s_z, s_x, s_b, s_c, s_dt,
     c_qa, c_kv, c_kr, c_z) = jnp.split(proj, split_at, axis=-1)
    y_a = short_conv_branch(a_h, a_b, a_c, a_z, conv_a_w)
    y_b = ssd_branch(s_z, s_x, s_b, s_c, s_dt, ssd_conv_w, ssd_conv_b, ssd_dt_bias,
                     ssd_a_log, ssd_d, ssd_norm_g)
    y_c = mla_branch(c_qa, c_kv, c_kr, c_z, cos, sin, mla_q_norm_g, w_qb, mla_kv_norm_g, w_kvb)
    y = jnp.concatenate([y_a, y_b, y_c], axis=-1)
    return x + jnp.einsum('bsm,md->bsd', y, w_out)


def setup_inputs(seed: int = 0) -> dict:
    key = jax.random.key(seed)
    ks = jax.random.split(key, 20)
    f32 = jnp.float32
    nrm = jax.random.normal
    x = nrm(ks[0], (BATCH, SEQ, D_MODEL), f32)
    offs = jax.random.randint(ks[1], (BATCH, 1), 0, POS_OFFSET_MAX, dtype=jnp.int32)
    positions = (offs + jnp.arange(SEQ, dtype=jnp.int32)[None, :]).astype(jnp.int32)
    norm_g = 1.0 + 0.02 * nrm(ks[2], (DEPTH, D_MODEL), f32)
    w_in = nrm(ks[3], (DEPTH, D_MODEL, IN_COLS), f32) * D_MODEL ** -0.5
    conv_a_w = nrm(ks[4], (DEPTH, CONV_A_WIDTH, D_CONV_A), f32) * CONV_A_WIDTH ** -0.5
    ssd_conv_w = nrm(ks[5], (DEPTH, SSD_CONV_WIDTH, SSD_CONV_DIM), f32) * SSD_CONV_WIDTH ** -0.5
    ssd_conv_b = 0.01 * nrm(ks[6], (DEPTH, SSD_CONV_DIM), f32)
    u = jax.random.uniform(ks[7], (DEPTH, SSD_HEADS), f32)
    dt0 = jnp.exp(u * (math.log(0.1) - math.log(0.001)) + math.log(0.001))
    ssd_dt_bias = dt0 + jnp.log(-jnp.expm1(-dt0))
    ssd_a_log = jnp.log(jax.random.uniform(ks[8], (DEPTH, SSD_HEADS), f32, 1.0, 16.0))
    ssd_d = 1.0 + 0.1 * nrm(ks[9], (DEPTH, SSD_HEADS), f32)
    ssd_norm_g = 1.0 + 0.02 * nrm(ks[10], (DEPTH, D_SSD), f32)
    mla_q_norm_g = 1.0 + 0.02 * nrm(ks[11], (DEPTH, Q_LORA), f32)
    w_qb = nrm(ks[12], (DEPTH, Q_LORA, MLA_HEADS * (QK_NOPE + QK_ROPE)), f32) * Q_LORA ** -0.5
    mla_kv_norm_g = 1.0 + 0.02 * nrm(ks[13], (DEPTH, KV_LORA), f32)
    w_kvb = nrm(ks[14], (DEPTH, KV_LORA, MLA_HEADS * (QK_NOPE + V_DIM)), f32) * KV_LORA ** -0.5
    w_out = nrm(ks[15], (DEPTH, D_MIX, D_MODEL), f32) * D_MIX ** -0.5
    final_norm_g = 1.0 + 0.02 * nrm(ks[16], (D_MODEL,), f32)
    return {'x': x, 'positions': positions, 'norm_g': norm_g, 'w_in': w_in, 'conv_a_w': conv_a_w,
            'ssd_conv_w': ssd_conv_w, 'ssd_conv_b': ssd_conv_b, 'ssd_dt_bias': ssd_dt_bias,
            'ssd_a_log': ssd_a_log, 'ssd_d': ssd_d, 'ssd_norm_g': ssd_norm_g,
            'mla_q_norm_g': mla_q_norm_g, 'w_qb': w_qb, 'mla_kv_norm_g': mla_kv_norm_g,
            'w_kvb': w_kvb, 'w_out': w_out, 'final_norm_g': final_norm_g}


def reference(x, positions, norm_g, w_in, conv_a_w, ssd_conv_w, ssd_conv_b, ssd_dt_bias,
              ssd_a_log, ssd_d, ssd_norm_g, mla_q_norm_g, w_qb, mla_kv_norm_g, w_kvb, w_out,
              final_norm_g):
    inv_freq = ROPE_BASE ** (-jnp.arange(0, QK_ROPE, 2, dtype=jnp.float32) / QK_ROPE)
    ang = positions.astype(jnp.float32)[..., None] * inv_freq
    cos, sin = jnp.cos(ang), jnp.sin(ang)
    for l in range(DEPTH):
        x = hybrid_layer(x, cos, sin, norm_g[l], w_in[l], conv_a_w[l], ssd_conv_w[l], ssd_conv_b[l],
                         ssd_dt_bias[l], ssd_a_log[l], ssd_d[l], ssd_norm_g[l], mla_q_norm_g[l],
                         w_qb[l], mla_kv_norm_g[l], w_kvb[l], w_out[l])
    return rmsnorm(x, final_norm_g)
```

```python
import math
from contextlib import ExitStack

import numpy as np
import concourse.bass as bass
import concourse.mybir as mybir
from concourse.bass_utils import run_bass_kernel_spmd

F32 = mybir.dt.float32
BF16 = mybir.dt.bfloat16
I32 = mybir.dt.int32
ALU = mybir.AluOpType
AF = mybir.ActivationFunctionType

PE, DVE, ACT, POOL, SP = "tensor", "vector", "scalar", "gpsimd", "sync"
ENGINES = (PE, DVE, ACT, POOL, SP)

DEPTH = 2
S_LEN = 2048
NTB = 4
D_MODEL = 1024
N_IN = 3142
NV = 58
V_NG, V_CA, V_SCW, V_SCB, V_SD, V_SNG, V_QNG, V_KNG = 0, 8, 14, 42, 49, 52, 55, 57
NEG = -30000.0
N_FILL = 1
STRICT_SAME_ENGINE = True
TWO_PI = 2.0 * math.pi


class _Op:
    __slots__ = ("eng", "fn", "deps", "is_dma", "sem", "count", "needs_inc")

    def __init__(self, eng, fn, is_dma):
        self.eng = eng
        self.fn = fn
        self.deps = []
        self.is_dma = is_dma
        self.sem = None
        self.count = 0
        self.needs_inc = is_dma


class Sched:
    N_DMA_SEMS = 6

    def __init__(self, nc):
        self.nc = nc
        self.ops = {e: [] for e in ENGINES}
        self.last_w = {}
        self.readers = {}
        self.dma_rr = {e: 0 for e in ENGINES}
        self.dma_last = {}
        self.marks = []

    def mark(self, name):
        self.marks.append((name, {e: len(self.ops[e]) for e in ENGINES}))

    def _add_dep(self, op, d, raw, rr=False):
        if d is None or d is op:
            return
        if rr and d.eng == op.eng:
            return
        if (not d.is_dma) and (not op.is_dma) and d.eng == op.eng:
            if op.eng == PE or not (raw or STRICT_SAME_ENGINE):
                return
        if d not in op.deps:
            op.deps.append(d)

    def _track(self, op, reads, writes):
        for k in reads:
            self._add_dep(op, self.last_w.get(k), True)
        for k in writes:
            rr = (op.eng != PE) and isinstance(k, tuple) and k[0] == "ps"
            self._add_dep(op, self.last_w.get(k), k in reads, rr)
            for r in self.readers.get(k, ()):
                self._add_dep(op, r, False, rr)
        for k in reads:
            self.readers.setdefault(k, []).append(op)
        for k in writes:
            self.last_w[k] = op
            self.readers[k] = []

    def inherit(self, new_key, old_keys):
        fr = []
        for k in old_keys:
            w = self.last_w.get(k)
            if w is not None and w not in fr:
                fr.append(w)
            for r in self.readers.get(k, ()):
                if r not in fr:
                    fr.append(r)
        self.readers[new_key] = fr
        self.last_w[new_key] = None

    def add_frontier(self, key, old_keys):
        fr = self.readers.setdefault(key, [])
        for k in old_keys:
            w = self.last_w.get(k)
            if w is not None and w not in fr:
                fr.append(w)
            for r in self.readers.get(k, ()):
                if r not in fr:
                    fr.append(r)

    def op(self, eng, fn, reads=(), writes=()):
        o = _Op(eng, fn, False)
        self._track(o, reads, writes)
        self.ops[eng].append(o)
        return o

    def dma(self, eng, out, in_, reads=(), writes=()):
        o = _Op(eng, lambda e: e.dma_start(out=out, in_=in_), True)
        slot = (eng, self.dma_rr[eng] % self.N_DMA_SEMS)
        self.dma_rr[eng] += 1
        prev = self.dma_last.get(slot)
        if prev is not None:
            o.deps.append(prev)
        o.sem = slot
        o.count = (prev.count if prev is not None else 0) + 16
        self.dma_last[slot] = o
        self._track(o, reads, writes)
        self.ops[eng].append(o)
        return o

    def finish(self, eng=SP):
        o = _Op(eng, None, False)
        for d in self.dma_last.values():
            o.deps.append(d)
        self.ops[eng].append(o)

    def emit(self):
        nc = self.nc
        for e in ENGINES:
            for o in self.ops[e]:
                for d in o.deps:
                    d.needs_inc = True
        for e in ENGINES:
            c = 0
            for o in self.ops[e]:
                if not o.is_dma:
                    if o.needs_inc:
                        c += 1
                    o.count = c
                    o.sem = e
        with ExitStack() as st:
            sems = {}
            for e in ENGINES:
                sems[e] = st.enter_context(nc.semaphore("c_" + e))
                for i in range(self.N_DMA_SEMS):
                    sems[(e, i)] = st.enter_context(nc.semaphore("d_%s%d" % (e, i)))
            block = st.enter_context(nc.Block())

            def run(eng_name):
                def body(eng):
                    waited = {}
                    for o in self.ops[eng_name]:
                        need = {}
                        for d in o.deps:
                            if d.count > need.get(d.sem, 0):
                                need[d.sem] = d.count
                        for sm, cnt in need.items():
                            if waited.get(sm, 0) >= cnt:
                                continue
                            eng.wait_ge(sems[sm], cnt)
                            waited[sm] = cnt
                        if o.fn is None:
                            continue
                        ins = o.fn(eng)
                        if o.is_dma:
                            ins.then_inc(sems[o.sem], 16)
                        elif o.needs_inc:
                            ins.then_inc(sems[o.sem], 1)

                return body

            block.tensor(run(PE))
            block.vector(run(DVE))
            block.scalar(run(ACT))
            block.gpsimd(run(POOL))
            block.sync(run(SP))


def build(nc, depth=DEPTH, stop_after=None, dbg=False):
    L = DEPTH
    dram = lambda name, shape, dt, kind="ExternalInput": nc.dram_tensor(name, shape, dt, kind=kind).ap()
    xT_d = dram("xT", [D_MODEL, S_LEN], F32)
    pos_d = dram("pos", [1, S_LEN], I32)
    win_d = dram("w_in", [L, D_MODEL, N_IN], F32)
    wout_d = dram("w_out", [L, D_MODEL, D_MODEL], F32)
    wqb_d = dram("w_qb", [L, 256, 768], F32)
    wkk_d = dram("w_kk", [L, 128, 768], F32)
    wkv_d = dram("w_kv", [L, 128, 384], F32)
    vec_d = dram("vecs", [L, 128, NV], F32)
    row_d = dram("rowv", [L, 1, 12], F32)
    fin_d = dram("fin_g", [128, 8], F32)
    cf_d = dram("cf", [128, 515], F32)
    cb_d = dram("cb", [128, 1024], F32)
    out_d = dram("outT", [D_MODEL, S_LEN], F32, kind="ExternalOutput")
    posf_d = dram("pos_f32", [1, S_LEN], F32, kind="Internal")
    if dbg:
        dbg_y = dram("dbg_y", [D_MODEL, S_LEN], BF16, kind="ExternalOutput")
        dbg_x = dram("dbg_x", [D_MODEL, S_LEN], F32, kind="ExternalOutput")

    S = Sched(nc)
    with ExitStack() as st:
        sb = lambda name, shape, dt: st.enter_context(nc.sbuf_tensor(name, shape, dt))
        xT = sb("xT_sb", [128, 8 * S_LEN], F32)
        xn = sb("xn_sb", [128, NTB * 8 * 512], BF16)
        yT = sb("yT_sb", [128, 8 * S_LEN], BF16)
        ring = [sb("ring%d" % i, [128, 8 * 512], BF16) for i in range(3)]
        wqb = sb("wqb_sb", [128, 2 * 768], BF16)
        wkk = sb("wkk_sb", [128, 768], BF16)
        wkv = sb("wkv_sb", [128, 384], BF16)
        vec = sb("vec_sb", [128, L * NV], F32)
        fin = sb("fin_sb", [128, 8], F32)
        rowb = sb("rowb_sb", [128, L * 12], F32)
        cf = sb("cf_sb", [128, 515], F32)
        cb = sb("cb_sb", [128, 1024], BF16)
        posi = sb("posi_sb", [128, 128], I32)
        SCR_W = 11600
        scr = sb("scr_sb", [128, SCR_W], F32)
        ps = [st.enter_context(nc.psum_tensor("ps%d" % i, [128, 512], F32)) for i in range(8)]
        psb = ps[7][:, :].bitcast(BF16)

        ident_f = cf[:, 0:128]
        tri_f = cf[:, 128:256]
        negm_f = cf[:, 256:384]
        ones_f = cf[:, 384:512]
        ident_b = cb[:, 0:128]
        negm_b = cb[:, 128:256]
        ones_b = cb[:, 256:384]
        gsel = [cb[:, 384 + i * 128:384 + (i + 1) * 128] for i in range(5)]

        def PK(i):
            return ("ps", i)

        def MM(out, lhsT, rhs, start, stop, reads, pkey, sgc=False):
            if sgc:
                S.op(PE, lambda e: e.matmul(out, lhsT=lhsT, rhs=rhs, start=start, stop=stop, skip_group_check=True),
                     reads=reads, writes=[pkey])
            else:
                S.op(PE, lambda e: e.matmul(out, lhsT=lhsT, rhs=rhs, start=start, stop=stop), reads=reads, writes=[pkey])

        def TR(out, in_, ident, reads, pkey):
            S.op(PE, lambda e: e.transpose(out=out, in_=in_, identity=ident), reads=reads, writes=[pkey])

        def AC(out, in_, func, reads, writes, scale=None, bias=None, eng=ACT):
            kw = {}
            if scale is not None:
                kw["scale"] = scale
            if bias is not None:
                kw["bias"] = bias
            S.op(eng, lambda e: e.activation(out=out, in_=in_, func=func, **kw), reads=reads, writes=writes)

        def TT(eng, out, in0, in1, op, reads, writes):
            S.op(eng, lambda e: e.tensor_tensor(out=out, in0=in0, in1=in1, op=op), reads=reads, writes=writes)

        def TS(eng, out, in0, s1, s2, op0, op1, reads, writes):
            if s2 is None:
                S.op(eng, lambda e: e.tensor_scalar(out=out, in0=in0, scalar1=s1, scalar2=None, op0=op0), reads=reads, writes=writes)
            else:
                S.op(eng, lambda e: e.tensor_scalar(out=out, in0=in0, scalar1=s1, scalar2=s2, op0=op0, op1=op1), reads=reads, writes=writes)

        def STT(eng, out, in0, scalar, in1, op0, op1, reads, writes):
            S.op(eng, lambda e: e.scalar_tensor_tensor(out=out, in0=in0, scalar=scalar, in1=in1, op0=op0, op1=op1), reads=reads, writes=writes)

        def CP(eng, out, in_, reads, writes):
            if eng == ACT:
                S.op(eng, lambda e: e.activation(out=out, in_=in_, func=AF.Copy), reads=reads, writes=writes)
            else:
                S.op(eng, lambda e: e.tensor_copy(out=out, in_=in_), reads=reads, writes=writes)

        def MS(eng, out, val, writes):
            S.op(eng, lambda e: e.memset(out, val), reads=(), writes=writes)

        class Scr:
            def __init__(self):
                self.off = 0
                self.keys = []
                self.old = []
                self.ph = 0

            def reset(self):
                self.old = self.old + self.keys
                self.keys = []
                self.off = 0
                self.ph += 1

            def f32(self, name, n):
                n2 = (n + 1) // 2 * 2
                a = scr[:, self.off:self.off + n]
                assert self.off + n2 <= SCR_W, (name, self.off, n2)
                self.off += n2
                k = (self.ph, name)
                self.keys.append(k)
                S.inherit(k, self.old)
                return a, k

            def bf16(self, name, n):
                w = (n + 3) // 4 * 2
                assert self.off + w <= SCR_W, (name, self.off, w)
                a = scr[:, self.off:self.off + w].bitcast(BF16)[:, 0:n]
                self.off += w
                k = (self.ph, name)
                self.keys.append(k)
                S.inherit(k, self.old)
                return a, k

        SC = Scr()

        S.dma(SP, cf[:], cf_d, writes=["cf"])
        S.dma(POOL, posf_d, pos_d, writes=["posf"])
        S.dma(POOL, cb[:], cb_d, writes=["cb"])
        S.dma(SP, fin[:], fin_d, writes=["fin"])
        for l in range(L):
            S.dma(SP, vec[:, l * NV:(l + 1) * NV], vec_d[l], writes=["vec"])
            S.dma(SP, rowb[:, l * 12:(l + 1) * 12], row_d[l].partition_broadcast(128), writes=["rowb"])
        xT3 = xT[:, :].rearrange("p (c t) -> p c t", c=8)
        for tb in range(NTB):
            S.dma(SP, xT3[:, :, tb * 512:(tb + 1) * 512],
                  xT_d[:, tb * 512:(tb + 1) * 512].rearrange("(c p) t -> p c t", p=128),
                  writes=[("xT", c, tb) for c in range(8)])

        def load_ring(src_ap, ncols, s, after=()):
            dst = ring[s][:, :].rearrange("p (kc n) -> p kc n", kc=8)[:, :, 0:ncols]
            S.dma(POOL, dst, src_ap.rearrange("(kc k) n -> k kc n", k=128), reads=list(after), writes=[("ring", s)])
            return s

        def W(s, kc, c0, n):
            return ring[s][:, kc * 512 + c0: kc * 512 + c0 + n]

        def XN(tb, kc, t0=0, n=512):
            o = (tb * 8 + kc) * 512 + t0
            return xn[:, o:o + n]

        def XT(c, t0=0, n=S_LEN):
            return xT[:, c * S_LEN + t0: c * S_LEN + t0 + n]

        def YT(c, t0=0, n=S_LEN):
            return yT[:, c * S_LEN + t0: c * S_LEN + t0 + n]

        def proj(psi, s, c0, m, tb, extra_reads=()):
            for kc in range(8):
                MM(ps[psi][0:m, :], W(s, kc, c0, m), XN(tb, kc), kc == 0, kc == 7,
                   [("ring", s), ("xn", tb)] + list(extra_reads), PK(psi))

        def rstd_from(out_sb, psi, inv_n, eps, okey, tmp, tkey):
            AC(tmp, ps[psi][:, :], AF.Ln, [], [PK(psi), tkey], scale=inv_n, bias=eps)
            AC(out_sb, tmp, AF.Exp, [tkey], [okey], scale=-0.5)

        def load_small(l):
            S.dma(POOL, wqb[:, :].rearrange("p (kc n) -> p kc n", kc=2),
                  wqb_d[l].rearrange("(kc k) n -> k kc n", k=128), writes=["wqb"])
            S.dma(POOL, wkk[:, :], wkk_d[l], writes=["wkk"])
            S.dma(POOL, wkv[:, :], wkv_d[l], writes=["wkv"])

        def norm_stats(tb, nb):
            (rstd, k_rstd), (lnt, k_lnt), sqs = nb
            pn = 6 + (tb % 2)
            for c in range(8):
                sq, ksq = sqs[c % 3]
                AC(sq, XT(c, tb * 512, 512), AF.Square, [("xT", c, tb)], [ksq])
                MM(ps[pn][:, :], ones_b, sq, c == 0, c == 7, ["cb", ksq], PK(pn))
            r_ = rstd[:, tb * 512:(tb + 1) * 512]
            AC(lnt, ps[pn][:, :], AF.Ln, [], [PK(pn), k_lnt], scale=1.0 / D_MODEL, bias=1e-6)
            AC(r_, lnt, AF.Exp, [k_lnt], [k_rstd], scale=-0.5)

        def norm_apply(tb, nb, gain_ap, final):
            (rstd, k_rstd), (lnt, k_lnt), sqs = nb
            r_ = rstd[:, tb * 512:(tb + 1) * 512]
            for c in range(8):
                if final:
                    STT(DVE, XT(c, tb * 512, 512), XT(c, tb * 512, 512), gain_ap(c), r_, ALU.mult, ALU.mult,
                        [("xT", c, tb), k_rstd, "fin"], [("xT", c, tb)])
                    S.dma(SP, out_d[c * 128:(c + 1) * 128, tb * 512:(tb + 1) * 512], XT(c, tb * 512, 512), reads=[("xT", c, tb)])
                else:
                    STT(DVE, XN(tb, c), XT(c, tb * 512, 512), gain_ap(c), r_, ALU.mult, ALU.mult,
                        [("xT", c, tb), k_rstd, "vec"], [("xn", tb)])

        def norm_tb(tb, nb, gain_ap, final):
            norm_stats(tb, nb)
            norm_apply(tb, nb, gain_ap, final)

        def norm_alloc(tag):
            return (SC.f32("rstd" + tag, S_LEN), SC.f32("lnt" + tag, 512), [SC.bf16("sq%s%d" % (tag, i), 512) for i in range(3)])

        sA = None
        sB2_pre = None
        for l in range(depth):
            vb = l * NV
            VC = lambda i: vec[:, vb + i: vb + i + 1]
            s0, s1, s2 = (l % 3), ((l + 1) % 3), ((l + 2) % 3)
            if l == 0:
                S.mark('L%d_P0' % l)
                SC.reset()
                nb0 = norm_alloc("P")
                gate_x = [("xT", c_, 2) for c_ in range(8)]
                sA = [load_ring(win_d[l][:, 0:512], 512, s0), load_ring(win_d[l][:, 512:1024], 512, s1, gate_x)]
                sB2_pre = load_ring(win_d[l][:, 2048:2310], 262, s2, gate_x)
                load_small(l)
                for tb in range(NTB):
                    norm_tb(tb, nb0, lambda c: vec[:, V_NG + c: V_NG + c + 1], False)

            S.mark('L%d_A' % l)
            SC.reset()
            ubuf, k_u = SC.f32("u", 2 + S_LEN)
            tA = [[SC.f32("tA%d_%d" % (i, b), 512) for i in range(4)] for b in range(2)]
            it = 0
            for j in range(2):
                s = sA[j]
                MS(DVE, ubuf[:, 0:2], 0.0, [k_u])
                for tb in range(NTB):
                    (t_sz, k_sz), (t_h, k_h), (t_g, k_g), (t_v, k_v) = tA[it % 2]
                    pb = (it % 2) * 4
                    it += 1
                    for u in range(4):
                        proj(pb + u if pb + u < 7 else 3, s, u * 128, 128, tb)
                    p_h, p_b, p_c, p_z = [pb + u if pb + u < 7 else 3 for u in range(4)]
                    AC(t_sz, ps[p_z][:, :], AF.Silu, [], [PK(p_z), k_sz])
                    AC(t_h, ps[p_h][:, :], AF.Copy, [], [PK(p_h), k_h])
                    o = 2 + tb * 512
                    TT(DVE, ubuf[:, o:o + 512], ps[p_c][:, :], t_h, ALU.mult, [k_h], [PK(p_c), k_u])
                    TT(DVE, t_g, ps[p_b][:, :], t_sz, ALU.mult, [k_sz], [PK(p_b), k_g])
                    TS(DVE, t_v, ubuf[:, o:o + 512], VC(V_CA + j * 3 + 2), None, ALU.mult, None, [k_u, "vec"], [k_v])
                    STT(DVE, t_v, ubuf[:, o - 1:o + 511], VC(V_CA + j * 3 + 1), t_v, ALU.mult, ALU.add, [k_u, k_v, "vec"], [k_v])
                    STT(DVE, t_v, ubuf[:, o - 2:o + 510], VC(V_CA + j * 3 + 0), t_v, ALU.mult, ALU.add, [k_u, k_v, "vec"], [k_v])
                    TT(DVE, YT(j, tb * 512, 512), t_v, t_g, ALU.mult, [k_v, k_g], [("yT", j)])
                if j == 0:
                    sB0_pre = load_ring(win_d[l][:, 1024:1536], 512, s0)
            if stop_after == "A":
                break

            S.mark('L%d_B' % l)
            SC.reset()
            sB = [sB0_pre, load_ring(win_d[l][:, 1536:2048], 512, s1), sB2_pre]
            ccmap = [(sB[0], 0), (sB[0], 128), (sB[0], 256), (sB[0], 384), (sB[1], 0), (sB[1], 128), (sB[1], 256)]
            zmap = [(sB[1], 384), (sB[2], 0), (sB[2], 128)]
            raws = [SC.f32("raw%d" % i, 515) for i in range(2)]
            yvx, k_yvx = SC.f32("yvx", 504)
            yv = scr[:, 0:1536]
            yvk = [raws[0][1], raws[1][1], k_yvx]
            halo, k_halo = SC.f32("halo", 24)
            xsT, k_xs = SC.f32("xsT", 3 * 512)
            dtts = [SC.f32("dtt%d" % i, 24) for i in range(2)]
            lats = [SC.f32("lat%d" % i, 24) for i in range(2)]
            cscs = [SC.f32("csc%d" % i, 24) for i in range(2)]
            Abr, k_A = SC.f32("Abr", 6)
            Rf, k_Rf = SC.f32("Rf", 768)
            Ebs = [SC.bf16("Eb%d" % i, 768) for i in range(2)]
            dLs = [SC.f32("dL%d" % i, 768) for i in range(2)]
            acc, k_acc = dLs[1][0][:, 0:512], dLs[1][1]
            tsz, k_tsz = dLs[0][0][:, 0:512], dLs[0][1]
            dsas = [SC.f32("dsa%d" % i, 6) for i in range(2)]
            Sf, k_Sf = SC.f32("Sf", 384)
            BT, k_BT = SC.bf16("BT", 1024)
            CT, k_CT = SC.bf16("CT", 1024)
            off_MT = SC.off
            MTs = [SC.bf16("MT%d" % i, 768) for i in range(2)]
            off_CdT = SC.off
            CdTs = [SC.bf16("CdT%d" % i, 768) for i in range(2)]
            accC, kaccC = scr[:, off_MT:off_MT + 512], [MTs[0][1], MTs[1][1]]
            tszC, ktszC = scr[:, off_CdT:off_CdT + 512], [CdTs[0][1], CdTs[1][1]]
            xds = [SC.bf16("xd%d" % i, 384) for i in range(2)]
            xdds = [SC.bf16("xdd%d" % i, 384) for i in range(2)]
            Btoks = [SC.bf16("Btok%d" % i, 256) for i in range(2)]
            cdbs = [SC.f32("cdb%d" % i, 6) for i in range(4)]
            Sb, k_Sb = SC.bf16("Sb", 384)
            sqB = [(Ebs[0][0][:, 0:512], Ebs[0][1]), (Ebs[1][0][:, 0:512], Ebs[1][1]), (Rf.bitcast(BF16)[:, 0:512], k_Rf)]
            gt = [(tsz, k_tsz), (acc, k_acc), (Rf[:, 256:768], k_Rf)]
            zbank = (5, 1, 7)
            plan = {0: [(0, ones_b), (1, gsel[0])], 1: [(0, gsel[1]), (1, gsel[2]), (2, gsel[3])],
                    2: [(1, gsel[4]), (2, ones_b)]}

            def gate_A(tbx):
                for j in range(3):
                    pz = zbank[j]
                    g_, kg_ = gt[j]
                    AC(g_, ps[pz][:, :], AF.Silu, [], [PK(pz), kg_])
                for j in range(3):
                    g_, kg_ = gt[j]
                    xs_j = xsT[:, j * 512:(j + 1) * 512]
                    STT(DVE, xs_j, xs_j, VC(V_SD + j), yv[:, j * 512:(j + 1) * 512], ALU.mult, ALU.add, [k_xs, "vec"] + yvk, [k_xs])
                    TT(DVE, xs_j, xs_j, g_, ALU.mult, [k_xs, kg_], [k_xs])
                    AC(sqB[j][0], xs_j, AF.Square, [k_xs], [sqB[j][1]])

            def gate_B1(tbx):
                for oc in range(3):
                    pn = 3 + oc
                    lst = plan[oc]
                    for i, (kc, g_ap) in enumerate(lst):
                        MM(ps[pn][:, :], g_ap, sqB[kc][0], i == 0, i == len(lst) - 1, ["cb", sqB[kc][1]], PK(pn))
                for oc in range(3):
                    pn = 3 + oc
                    g_, kg_ = gt[oc]
                    AC(g_, ps[pn][:, :], AF.Ln, [], [PK(pn), kg_], scale=1.0 / 192.0, bias=1e-5)
                    AC(g_, g_, AF.Exp, [kg_], [kg_], scale=-0.5)

            def gate_B2(tbx):
                for oc in range(3):
                    g_, kg_ = gt[oc]
                    STT(DVE, YT(2 + oc, tbx * 512, 512), xsT[:, oc * 512:(oc + 1) * 512], VC(V_SNG + oc), g_,
                        ALU.mult, ALU.mult, [k_xs, kg_, "vec"], [("yT", 2 + oc)])

            rb = l * 12
            AC(Abr, rowb[:, rb + 6:rb + 12], AF.Exp, ["rowb"], [k_A])
            TS(DVE, Abr, Abr, -1.0, None, ALU.mult, None, [k_A], [k_A])
            MS(DVE, Sf, 0.0, [k_Sf])
            MS(DVE, Sb, 0.0, [k_Sb])
            MS(DVE, halo, 0.0, [k_halo])
            ri = 0
            def dt_stage(tb):
                (dtt, k_dt), (lat, k_la), (csc, k_csc) = dtts[tb % 2], lats[tb % 2], cscs[tb % 2]
                for c in range(4):
                    for kc in range(8):
                        MM(ps[2][:, c * 8:c * 8 + 6], XN(tb, kc, c * 128, 128), W(sB[2], kc, 256, 6), kc == 0, kc == 7,
                           [("xn", tb), ("ring", sB[2])], PK(2))
                dt3 = dtt.rearrange("p (c h) -> p c h", c=4)
                TT(DVE, dt3, ps[2][:, 0:32].rearrange("p (c e) -> p c e", c=4)[:, :, 0:6],
                   rowb[:, rb:rb + 6].unsqueeze(1).to_broadcast([128, 4, 6]), ALU.add, ["rowb"], [PK(2), k_dt])
                AC(dtt, dtt, AF.Exp, [k_dt], [k_dt])
                AC(dtt, dtt, AF.Ln, [k_dt], [k_dt], bias=1.0)
                TT(DVE, lat.rearrange("p (c h) -> p c h", c=4), dt3, Abr.unsqueeze(1).to_broadcast([128, 4, 6]),
                   ALU.mult, [k_dt, k_A], [k_la])
                MM(ps[2][:, 64:88], tri_f, lat, True, True, ["cf", k_la], PK(2))
                CP(DVE, csc, ps[2][:, 64:88], [], [PK(2), k_csc])

            dt_stage(0)
            for tb in range(NTB):
                (dtt, k_dt), (lat, k_la), (csc, k_csc) = dtts[tb % 2], lats[tb % 2], cscs[tb % 2]
                S.mark('L%d_B%d_dt' % (l, tb))
                def decay(c):
                    gi = tb * 4 + c
                    (Eb, k_Eb), (dL, k_dL), (dsa, k_dsa) = Ebs[gi % 2], dLs[gi % 2], dsas[gi % 2]
                    cdb, k_cdb = cdbs[gi % 4]
                    la_c = lat[:, c * 6:(c + 1) * 6]
                    cs_c = csc[:, c * 6:(c + 1) * 6]
                    dL3 = dL.rearrange("p (h l) -> p h l", h=6)
                    TT(POOL, Rf.rearrange("p (h l) -> p h l", h=6), la_c.unsqueeze(2).to_broadcast([128, 6, 128]),
                       tri_f.unsqueeze(1).to_broadcast([128, 6, 128]), ALU.mult, [k_la, "cf"], [k_Rf])
                    MM(ps[3][:, 0:384], ones_f, Rf[:, 0:384], True, True, ["cf", k_Rf], PK(3))
                    MM(ps[4][:, 0:384], ones_f, Rf[:, 384:768], True, True, ["cf", k_Rf], PK(4))
                    for hb in range(2):
                        AC(Eb[:, hb * 384:(hb + 1) * 384], ps[3 + hb][:, 0:384], AF.Exp, [], [PK(3 + hb), k_Eb])
                    for hb in range(2):
                        for r in range(3):
                            MM(ps[3 + hb][:, r * 128:(r + 1) * 128], ident_f, negm_f, False, True, ["cf"], PK(3 + hb), sgc=True)
                    for hb in range(2):
                        pq = ps[3 + hb][:, 0:384].rearrange("p (h l) -> p h l", h=3)
                        TT(DVE, dL3[:, hb * 3:(hb + 1) * 3, :], pq,
                           cs_c[:, hb * 3:(hb + 1) * 3].unsqueeze(2).to_broadcast([128, 3, 128]), ALU.subtract,
                           [k_csc], [PK(3 + hb), k_dL])
                        TT(DVE, dsa[:, hb * 3:(hb + 1) * 3], pq[:, :, 127], cs_c[:, hb * 3:(hb + 1) * 3], ALU.subtract,
                           [k_csc], [PK(3 + hb), k_dsa])
                        AC(cdb[:, hb * 3:(hb + 1) * 3], pq[:, :, 127], AF.Exp, [], [PK(3 + hb), k_cdb])
                    AC(dL, dL, AF.Exp, [k_dL], [k_dL])
                    AC(dsa, dsa, AF.Exp, [k_dsa], [k_dsa])

                S.mark('L%d_B%d_conv' % (l, tb))
                cbank = (0, 1, 5, 6)

                corder = (3, 4, 5, 6, 0, 1, 2)

                def conv_proj(p):
                    sl, c0 = ccmap[corder[p]]
                    proj(cbank[p % 4], sl, c0, 128, tb)

                def conv_evac(p):
                    cc = corder[p]
                    raw, k_raw = raws[p % 2]
                    pidx = cbank[p % 4]
                    CP(POOL, raw[:, 0:3], halo[:, cc * 3:cc * 3 + 3], [k_halo], [k_raw])
                    AC(raw[:, 3:515], ps[pidx][:, :], AF.Copy, [], [PK(pidx), k_raw])
                    ac_, k_ac = (accC, kaccC) if p % 2 == 0 else (tszC, ktszC)
                    AC(ac_, ps[pidx][:, :], AF.Copy, ["vec"], [PK(pidx)] + k_ac, scale=VC(V_SCW + cc * 4 + 3))

                def conv_back(p):
                    cc = corder[p]
                    raw, k_raw = raws[p % 2]
                    ac_, k_ac = (accC, kaccC) if p % 2 == 0 else (tszC, ktszC)
                    wv = lambda k: VC(V_SCW + cc * 4 + k)
                    CP(POOL, halo[:, cc * 3:cc * 3 + 3], raw[:, 512:515], [k_raw], [k_halo])
                    STT(DVE, ac_, raw[:, 2:514], wv(2), ac_, ALU.mult, ALU.add, [k_raw, "vec"] + k_ac, k_ac)
                    STT(DVE, ac_, raw[:, 1:513], wv(1), ac_, ALU.mult, ALU.add, [k_raw, "vec"] + k_ac, k_ac)
                    STT(DVE, ac_, raw[:, 0:512], wv(0), ac_, ALU.mult, ALU.add, [k_raw, "vec"] + k_ac, k_ac)
                    if cc < 3:
                        dst, kd = xsT[:, cc * 512:(cc + 1) * 512], k_xs
                    elif cc < 5:
                        dst, kd = BT[:, (cc - 3) * 512:(cc - 2) * 512], k_BT
                    else:
                        dst, kd = CT[:, (cc - 5) * 512:(cc - 4) * 512], k_CT
                    AC(dst, ac_, AF.Silu, ["vec"] + k_ac, [kd], bias=VC(V_SCB + cc))

                if tb > 0:
                    S.mark('L%d_B%d_gate' % (l, tb - 1))
                    gate_A(tb - 1)
                conv_proj(0)
                conv_proj(1)
                conv_evac(0)
                if tb > 0:
                    gate_B1(tb - 1)
                for p in range(7):
                    if p + 2 < 7:
                        conv_proj(p + 2)
                    if p + 1 < 7:
                        conv_evac(p + 1)
                    conv_back(p)
                    if p == 0 and tb > 0:
                        gate_B2(tb - 1)
                decay(0)
                decay(1)
                if tb == NTB - 1:
                    sC0_pre = load_ring(win_d[l][:, 2310:2758], 448, s0)
                S.mark('L%d_B%d_chunks' % (l, tb))
                def front(c):
                    gi = tb * 4 + c
                    t0 = c * 128
                    (Eb, k_Eb), (dL, k_dL), (dsa, k_dsa) = Ebs[gi % 2], dLs[gi % 2], dsas[gi % 2]
                    (MT, k_MT), (CdT, k_CdT), (xd, k_xd), (xdd, k_xdd) = MTs[gi % 2], CdTs[gi % 2], xds[gi % 2], xdds[gi % 2]
                    pT = 5 if gi % 2 == 0 else 2
                    bo = (gi % 2) * 256
                    for g in range(2):
                        MM(ps[1][:, g * 128:(g + 1) * 128], BT[:, g * 512 + t0:g * 512 + t0 + 128],
                           CT[:, g * 512 + t0:g * 512 + t0 + 128], True, True, [k_BT, k_CT], PK(1))
                    for j in range(3):
                        TR(ps[pT][:, j * 128:(j + 1) * 128], xsT[:, j * 512 + t0:j * 512 + t0 + 128], ident_f, [k_xs, "cf"], PK(pT))
                    for g in range(2):
                        TR(psb[:, bo + g * 128:bo + (g + 1) * 128], BT[:, g * 512 + t0:g * 512 + t0 + 128], ident_b, [k_BT, "cb"], PK(7))
                    TT(DVE, MT.rearrange("p (g r l) -> p g r l", g=2, r=3),
                       ps[1][:, 0:256].rearrange("p (g l) -> p g l", g=2).unsqueeze(2).to_broadcast([128, 2, 3, 128]),
                       dL.rearrange("p (g r l) -> p g r l", g=2, r=3), ALU.mult, [k_dL], [PK(1), k_MT])
                    TT(DVE, xd.rearrange("p (h q) -> p h q", h=6), ps[pT][:, 0:384].rearrange("p (h q) -> p h q", h=6),
                       dtt[:, c * 6:(c + 1) * 6].unsqueeze(2).to_broadcast([128, 6, 64]), ALU.mult, [k_dt], [PK(pT), k_xd])
                    TT(DVE, CdT.rearrange("p (g r l) -> p g r l", g=2, r=3),
                       CT.rearrange("p (g t) -> p g t", g=2)[:, :, t0:t0 + 128].unsqueeze(2).to_broadcast([128, 2, 3, 128]),
                       Eb.rearrange("p (g r l) -> p g r l", g=2, r=3), ALU.mult, [k_CT, k_Eb], [k_CdT])
                    TT(DVE, xdd.rearrange("p (h q) -> p h q", h=6), xd.rearrange("p (h q) -> p h q", h=6),
                       dsa.unsqueeze(2).to_broadcast([128, 6, 64]), ALU.mult, [k_xd, k_dsa], [k_xdd])
                    Bt, k_Bt = Btoks[gi % 2]
                    CP(ACT, Bt, psb[:, bo:bo + 256], [], [PK(7), k_Bt])

                def back(c):
                    gi = tb * 4 + c
                    t0 = c * 128
                    (MT, k_MT), (CdT, k_CdT), (xd, k_xd), (xdd, k_xdd) = MTs[gi % 2], CdTs[gi % 2], xds[gi % 2], xdds[gi % 2]
                    cdb, k_cdb = cdbs[gi % 4]
                    Bt, k_Bt = Btoks[gi % 2]
                    for g in range(2):
                        MM(ps[0][:, g * 192:(g + 1) * 192], Bt[:, g * 128:(g + 1) * 128], xdd[:, g * 192:(g + 1) * 192],
                           True, True, [k_Bt, k_xdd], PK(0))
                    for h in range(6):
                        o_ap = ps[6][(h % 2) * 64:(h % 2) * 64 + 64, (h // 2) * 128:(h // 2) * 128 + 128]
                        MM(o_ap, xd[:, h * 64:(h + 1) * 64], MT[:, h * 128:(h + 1) * 128], True, False, [k_xd, k_MT], PK(6))
                        MM(o_ap, Sb[:, h * 64:(h + 1) * 64], CdT[:, h * 128:(h + 1) * 128], False, True, [k_Sb, k_CdT], PK(6))
                    Sf3 = Sf.rearrange("p (h q) -> p h q", h=6)
                    TT(DVE, Sf3, Sf3, cdb.unsqueeze(2).to_broadcast([128, 6, 64]), ALU.mult, [k_Sf, k_cdb], [k_Sf])
                    TT(DVE, Sf, Sf, ps[0][:, 0:384], ALU.add, [k_Sf], [PK(0), k_Sf])
                    CP(DVE, Sb, Sf, [k_Sf], [k_Sb])
                    CP(ACT, yv.rearrange("p (j t) -> p j t", j=3)[:, :, t0:t0 + 128],
                       ps[6][:, 0:384].rearrange("p (j l) -> p j l", j=3), [], [PK(6)] + yvk)

                front(0)
                for c in range(4):
                    if c + 2 < 4:
                        decay(c + 2)
                    if c + 1 < 4:
                        front(c + 1)
                    if c == 2:
                        for j in range(3):
                            sl, c0 = zmap[j]
                            proj(zbank[j], sl, c0, 128, tb)
                    back(c)
                if tb + 1 < NTB:
                    dt_stage(tb + 1)
            S.mark('L%d_Bgate_last' % l)
            gate_A(NTB - 1)
            gate_B1(NTB - 1)
            gate_B2(NTB - 1)
            sC = [sC0_pre, load_ring(win_d[l][:, 2758:3142], 384, s1)]
            sD = [load_ring(wout_d[l][:, 0:512], 512, s2), None]
            if stop_after == "B":
                break

            S.mark('L%d_C1' % l)
            SC.reset()
            rope, k_rope = SC.f32("rope", S_LEN)
            qa_nT, k_qan = SC.bf16("qa_nT", 2 * S_LEN)
            kv_nT, k_kvn = SC.bf16("kv_nT", S_LEN)
            KT, k_KT = SC.bf16("KT", S_LEN)
            Vaug, k_V = SC.bf16("Vaug", 16 * 128)
            off_c1 = SC.off
            qa_raw, k_qar = SC.f32("qa_raw", 1024)
            kv_raw, k_kvr = SC.f32("kv_raw", 512)
            tmpc, k_tmpc = SC.f32("tmpc", 512)
            tmpd, k_tmpd = SC.f32("tmpd", 512)
            sqC = [SC.bf16("sqC%d" % i, 512) for i in range(3)]
            tmpe, k_tmpe = SC.f32("tmpe", 512)
            tmpg, k_tmpg = SC.f32("tmpg", 512)
            def rope_build(tb):
                S.dma(SP, tmpe[0:64, :], posf_d[:, tb * 512:(tb + 1) * 512].partition_broadcast(64), reads=["posf"], writes=[k_tmpe])
                TS(DVE, tmpe[0:64, :], tmpe[0:64, :], cf[0:64, 512:513], cf[0:64, 513:514], ALU.mult, ALU.add, [k_tmpe, "cf"], [k_tmpe])
                rr = rope[0:64, tb * 512:(tb + 1) * 512]
                TS(DVE, rr, tmpe[0:64, :], 1.0 / TWO_PI, None, ALU.mult, None, [k_tmpe], [k_rope])
                for q in range(4):
                    CP(DVE, posi[0:64, :], rr[:, q * 128:(q + 1) * 128], [k_rope], ["posi"])
                    CP(DVE, rr[:, q * 128:(q + 1) * 128], posi[0:64, :], ["posi"], [k_rope])
                STT(DVE, tmpe[0:64, :], rr, -TWO_PI, tmpe[0:64, :], ALU.mult, ALU.add, [k_rope, k_tmpe], [k_tmpe])
                TS(DVE, rr, tmpe[0:64, :], math.pi, -TWO_PI, ALU.is_gt, ALU.mult, [k_tmpe], [k_rope])
                TT(DVE, rr, tmpe[0:64, :], rr, ALU.add, [k_tmpe, k_rope], [k_rope])
            MS(DVE, KT[32:64, :], 0.0, [k_KT])
            MS(DVE, Vaug.rearrange("p (t e) -> p t e", t=16)[:, :, 64:128], 1.0, [k_V])
            szk = []
            rope_build(0)
            for tb in range(NTB):
                qb_ = (0, 1) if tb % 2 == 0 else (6, 7)
                proj(qb_[0], sC[0], 0, 128, tb)
                proj(qb_[1], sC[0], 128, 128, tb)
                proj(3, sC[0], 256, 128, tb)
                for kc in range(2):
                    AC(qa_raw[:, kc * 512:(kc + 1) * 512], ps[qb_[kc]][:, :], AF.Copy, [], [PK(qb_[kc]), k_qar])
                    AC(sqC[kc][0], qa_raw[:, kc * 512:(kc + 1) * 512], AF.Square, [k_qar], [sqC[kc][1]])
                AC(kv_raw, ps[3][:, :], AF.Copy, [], [PK(3), k_kvr])
                AC(sqC[2][0], kv_raw, AF.Square, [k_kvr], [sqC[2][1]])
                for kc in range(2):
                    MM(ps[2][:, :], ones_b, sqC[kc][0], kc == 0, kc == 1, ["cb", sqC[kc][1]], PK(2))
                MM(ps[4][:, :], ones_b, sqC[2][0], True, True, ["cb", sqC[2][1]], PK(4))
                AC(tmpc, ps[2][:, :], AF.Ln, [], [PK(2), k_tmpc], scale=1.0 / 256.0, bias=1e-6)
                AC(tmpc, tmpc, AF.Exp, [k_tmpc], [k_tmpc], scale=-0.5)
                AC(tmpd, ps[4][:, :], AF.Ln, [], [PK(4), k_tmpd], scale=1.0 / 128.0, bias=1e-6)
                AC(tmpd, tmpd, AF.Exp, [k_tmpd], [k_tmpd], scale=-0.5)
                for kc in range(2):
                    STT(DVE, qa_nT[:, kc * S_LEN + tb * 512: kc * S_LEN + (tb + 1) * 512], qa_raw[:, kc * 512:(kc + 1) * 512],
                        VC(V_QNG + kc), tmpc, ALU.mult, ALU.mult, [k_qar, k_tmpc, "vec"], [k_qan])
                STT(DVE, kv_nT[:, tb * 512:(tb + 1) * 512], kv_raw, VC(V_KNG), tmpd, ALU.mult, ALU.mult,
                    [k_kvr, k_tmpd, "vec"], [k_kvn])
                if tb + 1 < NTB:
                    rope_build(tb + 1)
            for tb in range(NTB):
                rr = rope[0:64, tb * 512:(tb + 1) * 512]
                AC(rr, rr, AF.Sin, [k_rope], [k_rope])
            for tb in range(NTB):
                r_tb = rope[:, tb * 512:(tb + 1) * 512]
                pk3 = 5 if tb % 2 == 0 else 4
                proj(pk3, sC[0], 384, 64, tb)
                TT(DVE, tmpg[0:32, :], ps[pk3][0:32, :], r_tb[0:32, :], ALU.mult, [k_rope], [PK(pk3), k_tmpg])
                TT(DVE, tmpe[0:32, :], ps[pk3][32:64, :], r_tb[32:64, :], ALU.mult, [k_rope], [PK(pk3), k_tmpe])
                TT(DVE, KT[0:32, tb * 512:(tb + 1) * 512], tmpg[0:32, :], tmpe[0:32, :], ALU.add, [k_tmpg, k_tmpe], [k_KT])
            sD[1] = load_ring(wout_d[l][:, 512:1024], 512, s0)
            czb = (6, 7, 0, 1, 2, 3)
            for tb in range(NTB):
                for j in range(3):
                    proj(czb[(3 * tb + j) % 6], sC[1], j * 128, 128, tb)
                kz = ("sz", l, tb)
                S.inherit(kz, [("xn", tb)])
                szk.append(kz)
                szv = xn[:, tb * 4096: tb * 4096 + 3072].bitcast(F32)
                for j in range(3):
                    pz = czb[(3 * tb + j) % 6]
                    AC(szv[:, j * 512:(j + 1) * 512], ps[pz][:, :], AF.Silu, [], [PK(pz), kz])
            S.mark('L%d_C2' % l)
            SC.off = off_c1
            c1keys = [k_qar, k_kvr, k_tmpc, k_tmpd, k_tmpe, k_tmpg, sqC[0][1], sqC[1][1], sqC[2][1]]
            tq, k_tq = SC.f32("tq", 512)
            tr_, k_tr = SC.f32("tr", 512)
            rec, k_rec = SC.f32("rec", 512)
            for k_ in (k_tq, k_tr, k_rec):
                S.inherit(k_, c1keys)
            QTb, PTb = [], []
            for i in range(2):
                a = xn[:, i * 4096 + 3072: i * 4096 + 3584]
                k = ("QT", l, i)
                S.inherit(k, [("xn", i)])
                QTb.append((a, k))
                MS(DVE, a[32:64, :], 0.0, [k])
            for i in range(4):
                tbr = 2 + i // 2
                o = tbr * 4096 + 3072 + (i % 2) * 512
                k = ("PT", l, i)
                S.inherit(k, [("xn", tbr)])
                PTb.append((xn[:, o:o + 512], k))
            scale = (64 + 32) ** -0.5
            rs = sC[1]
            KT2 = ring[rs][:, 0:S_LEN]
            V2 = ring[rs][:, S_LEN:2 * S_LEN]
            k_KT2, k_V2 = ("KT2", l), ("V2", l)
            S.inherit(k_KT2, [("ring", rs)])
            S.inherit(k_V2, [("ring", rs)])
            CP(DVE, KT2[0:32, :], KT[0:32, :], [k_KT], [k_KT2])
            MS(DVE, KT2[32:64, :], 0.0, [k_KT2])
            MS(DVE, V2.rearrange("p (t e) -> p t e", t=16)[:, :, 64:128], 1.0, [k_V2])
            KTs = [(KT, k_KT), (KT2, k_KT2)]
            Vs = [(Vaug, k_V), (V2, k_V2)]
            blocks = [(h, qb) for h in range(6) for qb in range(NTB)]
            NKE = 5
            early = [(dc_, tb_, kc_) for half_ in range(2) for tb_ in range(NTB) for dc_ in range(4 * half_, 4 * half_ + 4) for kc_ in range(NKE)]
            early_i = [0]

            hb_rr = [0]

            def prep_head_piece(h, i):
                KTh, kK = KTs[h % 2]
                Vh, kV = Vs[h % 2]
                pk = (1, 2)[hb_rr[0] % 2]
                hb_rr[0] += 1
                if i < 4:
                    tb = i
                    MM(ps[pk][:, :], wkk[:, h * 128:(h + 1) * 128], kv_nT[:, tb * 512:(tb + 1) * 512], True, True,
                       ["wkk", k_kvn], PK(pk))
                    CP(DVE, KTh[64:128, tb * 512:(tb + 1) * 512], ps[pk][64:128, :], [], [PK(pk), kK])
                else:
                    half = i - 4
                    for t in range(8):
                        kt = half * 8 + t
                        MM(ps[pk][:, t * 64:(t + 1) * 64], kv_nT[:, kt * 128:(kt + 1) * 128], wkv[:, h * 64:(h + 1) * 64],
                           True, True, [k_kvn, "wkv"], PK(pk))
                    CP(DVE, Vh.rearrange("p (t e) -> p t e", t=16)[:, half * 8:(half + 1) * 8, 0:64],
                       ps[pk][:, :].rearrange("p (t e) -> p t e", t=8), [], [PK(pk), kV])

            def prep_head(h):
                for i in range(6):
                    prep_head_piece(h, i)

            def prep_q_pe(bi):
                h, qb = blocks[bi]
                for kc in range(2):
                    MM(ps[2][:, :], wqb[:, kc * 768 + h * 128: kc * 768 + (h + 1) * 128],
                       qa_nT[:, kc * S_LEN + qb * 512: kc * S_LEN + (qb + 1) * 512], kc == 0, kc == 1, ["wqb", k_qan], PK(2))

            def prep_q_ev(bi):
                h, qb = blocks[bi]
                QT, k_QT = QTb[bi % 2]
                r_qb = rope[:, qb * 512:(qb + 1) * 512]
                CP(DVE, QT[64:128, :], ps[2][64:128, :], [], [PK(2), k_QT])
                TT(DVE, tq[0:32, :], ps[2][0:32, :], r_qb[0:32, :], ALU.mult, [k_rope], [PK(2), k_tq])
                TT(DVE, tr_[0:32, :], ps[2][32:64, :], r_qb[32:64, :], ALU.mult, [k_rope], [PK(2), k_tr])
                TT(DVE, QT[0:32, :], tq[0:32, :], tr_[0:32, :], ALU.add, [k_tq, k_tr], [k_QT])

            def prep_q(bi):
                prep_q_pe(bi)
                prep_q_ev(bi)

            G = []
            for bi, (h, qb) in enumerate(blocks):
                for kt in range(4 * qb + 4):
                    G.append((bi, kt))

            def qk(g):
                bi, kt = G[g]
                h, qb = blocks[bi]
                QT, k_QT = QTb[bi % 2]
                KTh, kK = KTs[h % 2]
                j = kt - 4 * qb
                q0 = max(0, j) * 128
                pS = 3 + (g % 2)
                MM(ps[pS][:, q0:512], KTh[:, kt * 128:(kt + 1) * 128], QT[:, q0:512], True, j < 0, [kK, k_QT], PK(pS))
                if j >= 0:
                    MM(ps[pS][:, q0:q0 + 128], ident_b, negm_b, False, True, ["cb"], PK(pS))
                for _ in range(N_FILL):
                    if early_i[0] < len(early):
                        dc_, tb_, kc_ = early[early_i[0]]
                        bk_ = (0, 7)[(early_i[0] // NKE) % 2]
                        early_i[0] += 1
                        s_ = sD[dc_ // 4]
                        MM(ps[bk_][:, :], W(s_, kc_, (dc_ % 4) * 128, 128), YT(kc_, tb_ * 512, 512), kc_ == 0, kc_ == NKE - 1,
                           [("ring", s_), ("yT", kc_)], PK(bk_))
                        if kc_ == NKE - 1:
                            TT(DVE, XT(dc_, tb_ * 512, 512), XT(dc_, tb_ * 512, 512), ps[bk_][:, :], ALU.add,
                               [("xT", dc_, tb_)], [PK(bk_), ("xT", dc_, tb_)])
                    else:
                        MM(ps[7][:, :], ident_b, qa_nT[:, 0:512], True, True, ["cb", k_qan], PK(7))

            def pv(g):
                bi, kt = G[g]
                h, qb = blocks[bi]
                Vh, kV = Vs[h % 2]
                nk = 4 * qb + 4
                j = kt - 4 * qb
                q0 = max(0, j) * 128
                pS = 3 + (g % 2)
                po = 5 + (bi % 2)
                PT, k_PT = PTb[g % 4]
                AC(PT[:, q0:512], ps[pS][:, q0:512], AF.Exp, [], [PK(pS), k_PT], scale=scale)
                MM(ps[po][:, q0:512], Vh[:, kt * 128:(kt + 1) * 128], PT[:, q0:512], kt == 0, kt == nk - 1,
                   [kV, k_PT], PK(po))
                if kt == 1 and pend_fin:
                    pend_fin.pop(0)()
                if kt == nk - 1:
                    def fin(h=h, qb=qb, po=po):
                        AC(rec[0:64, :], ps[po][64:128, :], AF.Ln, [], [PK(po), k_rec])
                        AC(rec[0:64, :], rec[0:64, :], AF.Exp, [k_rec], [k_rec], scale=-1.0)
                        hp, hr = h // 2, (h % 2) * 64
                        TT(DVE, tq[hr:hr + 64, :], ps[po][0:64, :], rec[0:64, :], ALU.mult, [k_rec], [PK(po), k_tq])
                        szv = xn[:, qb * 4096: qb * 4096 + 3072].bitcast(F32)
                        TT(DVE, YT(5 + hp, qb * 512, 512)[hr:hr + 64, :], tq[hr:hr + 64, :], szv[hr:hr + 64, hp * 512:(hp + 1) * 512],
                           ALU.mult, [k_tq, szk[qb]], [("yT", 5 + hp)])
                    pend_fin.append(fin)

            pend_fin = []
            prep_head(0)
            prep_q(0)
            for g in range(len(G) + 1):
                if g < len(G):
                    bi, kt = G[g]
                    h, qb = blocks[bi]
                    if kt == 0 and bi + 1 < len(blocks):
                        prep_q_pe(bi + 1)
                    if kt == 1 and bi + 1 < len(blocks):
                        prep_q_ev(bi + 1)
                    if h + 1 < 6:
                        if qb == 1 and kt in (3, 5, 7):
                            prep_head_piece(h + 1, (kt - 3) // 2)
                        if qb == 2 and kt in (3, 5, 7):
                            prep_head_piece(h + 1, 3 + (kt - 3) // 2)
                    qk(g)
                if g >= 1:
                    pv(g - 1)
            while pend_fin:
                pend_fin.pop(0)()
            S.add_frontier(("ring", rs), [k_KT2, k_V2])
            for i in range(2):
                S.add_frontier(("xn", i), [szk[i], QTb[i][1]])
            for i in range(4):
                S.add_frontier(("xn", 2 + i // 2), [szk[2 + i // 2], PTb[i][1]])
            if stop_after == "C":
                break

            S.mark('L%d_D' % l)
            SC.reset()
            nbn = norm_alloc("N%d" % l)
            last = (l == DEPTH - 1)
            if not last and l + 1 < depth:
                sA = [load_ring(win_d[l + 1][:, 0:512], 512, s1), None]
                load_small(l + 1)
            pi = 0
            pend_apply = []
            for tb in range(NTB):
                for dc in range(8):
                    s_ = sD[dc // 4]
                    c0 = (dc % 4) * 128
                    pp = pi % 6
                    pi += 1
                    kcs = list(range(NKE, 8)) if early_i[0] >= len(early) else list(range(8))
                    for kc in kcs:
                        MM(ps[pp][:, :], W(s_, kc, c0, 128), YT(kc, tb * 512, 512), kc == kcs[0], kc == kcs[-1],
                           [("ring", s_), ("yT", kc)], PK(pp))
                    TT(DVE, XT(dc, tb * 512, 512), XT(dc, tb * 512, 512), ps[pp][:, :], ALU.add, [("xT", dc, tb)],
                       [PK(pp), ("xT", dc, tb)])
                if stop_after is None and (last or l + 1 < depth):
                    if pend_apply:
                        pend_apply.pop(0)()
                    norm_stats(tb, nbn)
                    if last:
                        pend_apply.append((lambda tb_: (lambda: norm_apply(tb_, nbn, lambda c: fin[:, c:c + 1], True)))(tb))
                    else:
                        vb2 = (l + 1) * NV
                        gfn = (lambda vb2_: (lambda c: vec[:, vb2_ + V_NG + c: vb2_ + V_NG + c + 1]))(vb2)
                        pend_apply.append((lambda tb_, g_: (lambda: norm_apply(tb_, nbn, g_, False)))(tb, gfn))
            while pend_apply:
                pend_apply.pop(0)()

            if not last and l + 1 < depth:
                sA[1] = load_ring(win_d[l + 1][:, 512:1024], 512, s2)
                sB2_pre = load_ring(win_d[l + 1][:, 2048:2310], 262, s0)

        if stop_after is not None or depth < DEPTH:
            for c in range(8):
                S.dma(SP, out_d[c * 128:(c + 1) * 128, :], XT(c), reads=[("xT", c, tb) for tb in range(NTB)])
        if dbg:
            for c in range(8):
                S.dma(SP, dbg_y[c * 128:(c + 1) * 128, :], YT(c), reads=[("yT", c)])
        S.finish(SP)
        S.mark('END')
        S.emit()
    nc._marks = S.marks
    return nc


_SPLIT = (256, 256, 256, 256, 384, 384, 256, 256, 6, 256, 128, 32, 384)


def _pack(inputs):
    L = DEPTH
    f = np.float32
    w_in = np.asarray(inputs["w_in"], f)
    offs = np.concatenate([[0], np.cumsum(_SPLIT)])
    seg = {n: (int(offs[i]), int(offs[i + 1])) for i, n in enumerate(
        ["a_h", "a_b", "a_c", "a_z", "s_z", "s_x", "s_b", "s_c", "s_dt", "c_qa", "c_kv", "c_kr", "c_z"])}

    def cols(n, a=None, b=None):
        s0, s1 = seg[n]
        a = 0 if a is None else a
        b = (s1 - s0) if b is None else b
        return np.arange(s0 + a, s0 + b)

    idx = np.concatenate([
        cols("a_h", 0, 128), cols("a_b", 0, 128), cols("a_c", 0, 128), cols("a_z", 0, 128),
        cols("a_h", 128, 256), cols("a_b", 128, 256), cols("a_c", 128, 256), cols("a_z", 128, 256),
        cols("s_x"), cols("s_b"), cols("s_c"), cols("s_z"), cols("s_dt"),
        cols("c_qa"), cols("c_kv"), cols("c_kr"), cols("c_kr", 16, 32), cols("c_kr", 0, 16), cols("c_z")])
    assert idx.shape[0] == N_IN
    w_in_p = np.ascontiguousarray(w_in[:, :, idx])
    w_qb = np.asarray(inputs["w_qb"], f)
    qidx = []
    for h in range(6):
        b0 = h * 96
        qidx += list(range(b0 + 64, b0 + 96)) + list(range(b0 + 80, b0 + 96)) + list(range(b0 + 64, b0 + 80)) + list(range(b0, b0 + 64))
    w_qb_p = np.ascontiguousarray(w_qb[:, :, np.array(qidx)])
    w_kvb = np.asarray(inputs["w_kvb"], f)
    w_kk = np.zeros((L, 128, 768), f)
    w_kv = np.zeros((L, 128, 384), f)
    for h in range(6):
        w_kk[:, :, h * 128 + 64:(h + 1) * 128] = w_kvb[:, :, h * 128:h * 128 + 64]
        w_kv[:, :, h * 64:(h + 1) * 64] = w_kvb[:, :, h * 128 + 64:(h + 1) * 128]
    vecs = np.zeros((L, 128, NV), f)
    colmaj = lambda v: np.asarray(v, f).reshape(-1, 128).T
    for l in range(L):
        vecs[l, :, V_NG:V_NG + 8] = colmaj(inputs["norm_g"][l])
        ca = np.asarray(inputs["conv_a_w"][l], f)
        for j in range(2):
            for k in range(3):
                vecs[l, :, V_CA + j * 3 + k] = ca[k, j * 128:(j + 1) * 128]
        sw = np.asarray(inputs["ssd_conv_w"][l], f)
        for cc in range(7):
            for k in range(4):
                vecs[l, :, V_SCW + cc * 4 + k] = sw[k, cc * 128:(cc + 1) * 128]
        vecs[l, :, V_SCB:V_SCB + 7] = colmaj(inputs["ssd_conv_b"][l])
        vecs[l, :, V_SD:V_SD + 3] = colmaj(np.repeat(np.asarray(inputs["ssd_d"][l], f), 64))
        vecs[l, :, V_SNG:V_SNG + 3] = colmaj(inputs["ssd_norm_g"][l])
        vecs[l, :, V_QNG:V_QNG + 2] = colmaj(inputs["mla_q_norm_g"][l])
        vecs[l, :, V_KNG:V_KNG + 1] = colmaj(inputs["mla_kv_norm_g"][l])
    rowv = np.concatenate([np.asarray(inputs["ssd_dt_bias"], f), np.asarray(inputs["ssd_a_log"], f)], axis=1).reshape(L, 1, 12)
    fin = np.ascontiguousarray(colmaj(inputs["final_norm_g"]))
    k = np.arange(128)
    cfm = np.zeros((128, 515), f)
    cfm[:, 0:128] = np.eye(128, dtype=f)
    cfm[:, 128:256] = (k[:, None] <= k[None, :]).astype(f)
    cfm[:, 256:384] = np.where(k[:, None] <= k[None, :], 0.0, NEG).astype(f)
    cfm[:, 384:512] = 1.0
    inv_freq = (10000.0 ** (-np.arange(0, 32, 2, dtype=np.float32) / np.float32(32))).astype(f)
    p = np.arange(64)
    cfm[0:64, 512] = inv_freq[p % 16]
    ph = np.where(p < 32, 0.5 * math.pi, np.where(p < 48, math.pi, 0.0))
    cfm[0:64, 513] = ph.astype(f)
    cfm[:, 514] = -math.pi
    cbm = np.zeros((128, 1024), f)
    cbm[:, 0:128] = np.eye(128, dtype=f)
    cbm[:, 128:256] = cfm[:, 256:384]
    cbm[:, 256:384] = 1.0
    lo = (k < 64)
    gB = np.repeat(lo[:, None], 128, 1)
    gC = np.repeat(lo[None, :], 128, 0)
    gD = (lo[:, None] == lo[None, :])
    gE = ~gC
    gF = ~gB
    for i, g in enumerate([gB, gC, gD, gE, gF]):
        cbm[:, 384 + i * 128:384 + (i + 1) * 128] = g.astype(f)
    shared = {"w_in": w_in_p, "w_out": np.ascontiguousarray(np.asarray(inputs["w_out"], f)), "w_qb": w_qb_p,
              "w_kk": w_kk, "w_kv": w_kv, "vecs": vecs, "rowv": rowv, "fin_g": fin, "cf": cfm, "cb": cbm}
    x = np.asarray(inputs["x"], f)
    pos = np.asarray(inputs["positions"], np.int32)
    maps = []
    for b in range(8):
        m = dict(shared)
        m["xT"] = np.ascontiguousarray(x[b].T)
        m["pos"] = np.ascontiguousarray(pos[b].reshape(1, S_LEN))
        maps.append(m)
    return maps


def kernel(**inputs):
    maps = _pack(inputs)
    nc = bass.Bass("TRN2", target_bir_lowering=False)
    build(nc)
    res = run_bass_kernel_spmd(nc, maps, core_ids=list(range(8)))
    out = np.stack([np.ascontiguousarray(np.asarray(r["outT"]).T) for r in res.results], axis=0)
    return out.astype(np.float32)
```

```python
import math
from contextlib import ExitStack

import numpy as np
import concourse.bass as bass
import concourse.mybir as mybir
from concourse.bass_utils import run_bass_kernel_spmd

F32 = mybir.dt.float32
BF16 = mybir.dt.bfloat16
I32 = mybir.dt.int32
ALU = mybir.AluOpType
AF = mybir.ActivationFunctionType

PE, DVE, ACT, POOL, SP = "tensor", "vector", "scalar", "gpsimd", "sync"
ENGINES = (PE, DVE, ACT, POOL, SP)

DEPTH = 2
S_LEN = 2048
NTB = 4
D_MODEL = 1024
N_IN = 3142
NV = 58
V_NG, V_CA, V_SCW, V_SCB, V_SD, V_SNG, V_QNG, V_KNG = 0, 8, 14, 42, 49, 52, 55, 57
NEG = -30000.0
N_FILL = 1
STRICT_SAME_ENGINE = True
TWO_PI = 2.0 * math.pi


class _Op:
    __slots__ = ("eng", "fn", "deps", "is_dma", "sem", "count", "needs_inc")

    def __init__(self, eng, fn, is_dma):
        self.eng = eng
        self.fn = fn
        self.deps = []
        self.is_dma = is_dma
        self.sem = None
        self.count = 0
        self.needs_inc = is_dma


class Sched:
    N_DMA_SEMS = 6

    def __init__(self, nc):
        self.nc = nc
        self.ops = {e: [] for e in ENGINES}
        self.last_w = {}
        self.readers = {}
        self.dma_rr = {e: 0 for e in ENGINES}
        self.dma_last = {}
        self.marks = []

    def mark(self, name):
        self.marks.append((name, {e: len(self.ops[e]) for e in ENGINES}))

    def _add_dep(self, op, d, raw, rr=False):
        if d is None or d is op:
            return
        if rr and d.eng == op.eng:
            return
        if (not d.is_dma) and (not op.is_dma) and d.eng == op.eng:
            if op.eng == PE or not (raw or STRICT_SAME_ENGINE):
                return
        if d not in op.deps:
            op.deps.append(d)

    def _track(self, op, reads, writes):
        for k in reads:
            self._add_dep(op, self.last_w.get(k), True)
        for k in writes:
            rr = (op.eng != PE) and isinstance(k, tuple) and k[0] == "ps"
            lw = self.last_w.get(k)
            rd = self.readers.get(k, ())
            implied = (lw is not None and len(rd) > 0 and (k not in reads) and (not lw.is_dma) and (not op.is_dma)
                       and lw.eng == op.eng and not rr)
            if not implied:
                self._add_dep(op, lw, k in reads, rr)
            for r in rd:
                self._add_dep(op, r, False, rr)
        for k in reads:
            self.readers.setdefault(k, []).append(op)
        for k in writes:
            self.last_w[k] = op
            self.readers[k] = []

    def inherit(self, new_key, old_keys):
        fr = []
        for k in old_keys:
            w = self.last_w.get(k)
            if w is not None and w not in fr:
                fr.append(w)
            for r in self.readers.get(k, ()):
                if r not in fr:
                    fr.append(r)
        self.readers[new_key] = fr
        self.last_w[new_key] = None

    def add_frontier(self, key, old_keys):
        fr = self.readers.setdefault(key, [])
        for k in old_keys:
            w = self.last_w.get(k)
            if w is not None and w not in fr:
                fr.append(w)
            for r in self.readers.get(k, ()):
                if r not in fr:
                    fr.append(r)

    def op(self, eng, fn, reads=(), writes=()):
        o = _Op(eng, fn, False)
        self._track(o, reads, writes)
        self.ops[eng].append(o)
        return o

    def dma(self, eng, out, in_, reads=(), writes=()):
        o = _Op(eng, lambda e: e.dma_start(out=out, in_=in_), True)
        slot = (eng, self.dma_rr[eng] % self.N_DMA_SEMS)
        self.dma_rr[eng] += 1
        prev = self.dma_last.get(slot)
        if prev is not None:
            o.deps.append(prev)
        o.sem = slot
        o.count = (prev.count if prev is not None else 0) + 16
        self.dma_last[slot] = o
        self._track(o, reads, writes)
        self.ops[eng].append(o)
        return o

    def finish(self, eng=SP):
        o = _Op(eng, None, False)
        for d in self.dma_last.values():
            o.deps.append(d)
        self.ops[eng].append(o)

    def emit(self):
        nc = self.nc
        for e in ENGINES:
            for o in self.ops[e]:
                for d in o.deps:
                    d.needs_inc = True
        for e in ENGINES:
            c = 0
            for o in self.ops[e]:
                if not o.is_dma:
                    if o.needs_inc:
                        c += 1
                    o.count = c
                    o.sem = e
        with ExitStack() as st:
            sems = {}
            for e in ENGINES:
                sems[e] = st.enter_context(nc.semaphore("c_" + e))
                for i in range(self.N_DMA_SEMS):
                    sems[(e, i)] = st.enter_context(nc.semaphore("d_%s%d" % (e, i)))
            block = st.enter_context(nc.Block())

            def run(eng_name):
                def body(eng):
                    waited = {}
                    for o in self.ops[eng_name]:
                        need = {}
                        for d in o.deps:
                            if d.count > need.get(d.sem, 0):
                                need[d.sem] = d.count
                        for sm, cnt in need.items():
                            if waited.get(sm, 0) >= cnt:
                                continue
                            eng.wait_ge(sems[sm], cnt)
                            waited[sm] = cnt
                        if o.fn is None:
                            continue
                        ins = o.fn(eng)
                        if o.is_dma:
                            ins.then_inc(sems[o.sem], 16)
                        elif o.needs_inc:
                            ins.then_inc(sems[o.sem], 1)

                return body

            block.tensor(run(PE))
            block.vector(run(DVE))
            block.scalar(run(ACT))
            block.gpsimd(run(POOL))
            block.sync(run(SP))


def build(nc, depth=DEPTH, stop_after=None, dbg=False):
    L = DEPTH
    dram = lambda name, shape, dt, kind="ExternalInput": nc.dram_tensor(name, shape, dt, kind=kind).ap()
    xT_d = dram("xT", [D_MODEL, S_LEN], F32)
    pos_d = dram("pos", [1, S_LEN], I32)
    win_d = dram("w_in", [L, D_MODEL, N_IN], F32)
    wout_d = dram("w_out", [L, D_MODEL, D_MODEL], F32)
    wqb_d = dram("w_qb", [L, 256, 768], F32)
    wkk_d = dram("w_kk", [L, 128, 768], F32)
    wkv_d = dram("w_kv", [L, 128, 384], F32)
    vec_d = dram("vecs", [L, 128, NV], F32)
    row_d = dram("rowv", [L, 1, 12], F32)
    fin_d = dram("fin_g", [128, 8], F32)
    cf_d = dram("cf", [128, 515], F32)
    cb_d = dram("cb", [128, 1024], F32)
    out_d = dram("outT", [D_MODEL, S_LEN], F32, kind="ExternalOutput")
    posf_d = dram("pos_f32", [1, S_LEN], F32, kind="Internal")
    if dbg:
        dbg_y = dram("dbg_y", [D_MODEL, S_LEN], BF16, kind="ExternalOutput")
        dbg_x = dram("dbg_x", [D_MODEL, S_LEN], F32, kind="ExternalOutput")

    S = Sched(nc)
    with ExitStack() as st:
        sb = lambda name, shape, dt: st.enter_context(nc.sbuf_tensor(name, shape, dt))
        xT = sb("xT_sb", [128, 8 * S_LEN], F32)
        xn = sb("xn_sb", [128, NTB * 8 * 512], BF16)
        yT = sb("yT_sb", [128, 8 * S_LEN], BF16)
        ring = [sb("ring%d" % i, [128, 8 * 512], BF16) for i in range(3)]
        wqb = sb("wqb_sb", [128, 2 * 768], BF16)
        wkk = sb("wkk_sb", [128, 768], BF16)
        wkv = sb("wkv_sb", [128, 384], BF16)
        vec = sb("vec_sb", [128, L * NV], F32)
        fin = sb("fin_sb", [128, 8], F32)
        rowb = sb("rowb_sb", [128, L * 12], F32)
        cf = sb("cf_sb", [128, 515], F32)
        cb = sb("cb_sb", [128, 1024], BF16)
        posi = sb("posi_sb", [128, 128], I32)
        SCR_W = 11600
        scr = sb("scr_sb", [128, SCR_W], F32)
        ps = [st.enter_context(nc.psum_tensor("ps%d" % i, [128, 512], F32)) for i in range(8)]
        psb = ps[7][:, :].bitcast(BF16)

        ident_f = cf[:, 0:128]
        tri_f = cf[:, 128:256]
        negm_f = cf[:, 256:384]
        ones_f = cf[:, 384:512]
        ident_b = cb[:, 0:128]
        negm_b = cb[:, 128:256]
        ones_b = cb[:, 256:384]
        gsel = [cb[:, 384 + i * 128:384 + (i + 1) * 128] for i in range(5)]

        def PK(i):
            return ("ps", i)

        def MM(out, lhsT, rhs, start, stop, reads, pkey, sgc=False):
            if sgc:
                S.op(PE, lambda e: e.matmul(out, lhsT=lhsT, rhs=rhs, start=start, stop=stop, skip_group_check=True),
                     reads=reads, writes=[pkey])
            else:
                S.op(PE, lambda e: e.matmul(out, lhsT=lhsT, rhs=rhs, start=start, stop=stop), reads=reads, writes=[pkey])

        def TR(out, in_, ident, reads, pkey):
            S.op(PE, lambda e: e.transpose(out=out, in_=in_, identity=ident), reads=reads, writes=[pkey])

        def AC(out, in_, func, reads, writes, scale=None, bias=None, eng=ACT):
            kw = {}
            if scale is not None:
                kw["scale"] = scale
            if bias is not None:
                kw["bias"] = bias
            S.op(eng, lambda e: e.activation(out=out, in_=in_, func=func, **kw), reads=reads, writes=writes)

        def TT(eng, out, in0, in1, op, reads, writes):
            S.op(eng, lambda e: e.tensor_tensor(out=out, in0=in0, in1=in1, op=op), reads=reads, writes=writes)

        def TS(eng, out, in0, s1, s2, op0, op1, reads, writes):
            if s2 is None:
                S.op(eng, lambda e: e.tensor_scalar(out=out, in0=in0, scalar1=s1, scalar2=None, op0=op0), reads=reads, writes=writes)
            else:
                S.op(eng, lambda e: e.tensor_scalar(out=out, in0=in0, scalar1=s1, scalar2=s2, op0=op0, op1=op1), reads=reads, writes=writes)

        def STT(eng, out, in0, scalar, in1, op0, op1, reads, writes):
            S.op(eng, lambda e: e.scalar_tensor_tensor(out=out, in0=in0, scalar=scalar, in1=in1, op0=op0, op1=op1), reads=reads, writes=writes)

        def CP(eng, out, in_, reads, writes):
            if eng == ACT:
                S.op(eng, lambda e: e.activation(out=out, in_=in_, func=AF.Copy), reads=reads, writes=writes)
            else:
                S.op(eng, lambda e: e.tensor_copy(out=out, in_=in_), reads=reads, writes=writes)

        def MS(eng, out, val, writes):
            S.op(eng, lambda e: e.memset(out, val), reads=(), writes=writes)

        class Scr:
            def __init__(self):
                self.off = 0
                self.keys = []
                self.old = []
                self.ph = 0

            def reset(self):
                self.old = self.old + self.keys
                self.keys = []
                self.off = 0
                self.ph += 1

            def f32(self, name, n):
                n2 = (n + 1) // 2 * 2
                a = scr[:, self.off:self.off + n]
                assert self.off + n2 <= SCR_W, (name, self.off, n2)
                self.off += n2
                k = (self.ph, name)
                self.keys.append(k)
                S.inherit(k, self.old)
                return a, k

            def bf16(self, name, n):
                w = (n + 3) // 4 * 2
                assert self.off + w <= SCR_W, (name, self.off, w)
                a = scr[:, self.off:self.off + w].bitcast(BF16)[:, 0:n]
                self.off += w
                k = (self.ph, name)
                self.keys.append(k)
                S.inherit(k, self.old)
                return a, k

        SC = Scr()

        S.dma(SP, cf[:], cf_d, writes=["cf"])
        S.dma(POOL, posf_d, pos_d, writes=["posf"])
        S.dma(POOL, cb[:], cb_d, writes=["cb"])
        S.dma(SP, fin[:], fin_d, writes=["fin"])
        for l in range(L):
            S.dma(SP, vec[:, l * NV:(l + 1) * NV], vec_d[l], writes=["vec"])
            S.dma(SP, rowb[:, l * 12:(l + 1) * 12], row_d[l].partition_broadcast(128), writes=["rowb"])
        for tb in range(NTB):
            for c in range(8):
                S.dma(SP, xT[:, c * S_LEN + tb * 512:c * S_LEN + (tb + 1) * 512],
                      xT_d[c * 128:(c + 1) * 128, tb * 512:(tb + 1) * 512], writes=[("xT", c, tb)])

        def load_ring(src_ap, ncols, s, after=()):
            dst = ring[s][:, :].rearrange("p (kc n) -> p kc n", kc=8)[:, :, 0:ncols]
            S.dma(POOL, dst, src_ap.rearrange("(kc k) n -> k kc n", k=128), reads=list(after), writes=[("ring", s)])
            return s

        def W(s, kc, c0, n):
            return ring[s][:, kc * 512 + c0: kc * 512 + c0 + n]

        def XN(tb, kc, t0=0, n=512):
            o = (tb * 8 + kc) * 512 + t0
            return xn[:, o:o + n]

        def XT(c, t0=0, n=S_LEN):
            return xT[:, c * S_LEN + t0: c * S_LEN + t0 + n]

        def YT(c, t0=0, n=S_LEN):
            return yT[:, c * S_LEN + t0: c * S_LEN + t0 + n]

        def proj(psi, s, c0, m, tb, extra_reads=()):
            for kc in range(8):
                MM(ps[psi][0:m, :], W(s, kc, c0, m), XN(tb, kc), kc == 0, kc == 7,
                   [("ring", s), ("xn", tb)] + list(extra_reads), PK(psi))

        def rstd_from(out_sb, psi, inv_n, eps, okey, tmp, tkey):
            AC(tmp, ps[psi][:, :], AF.Ln, [], [PK(psi), tkey], scale=inv_n, bias=eps)
            AC(out_sb, tmp, AF.Exp, [tkey], [okey], scale=-0.5)

        def load_small(l):
            S.dma(POOL, wqb[:, :].rearrange("p (kc n) -> p kc n", kc=2),
                  wqb_d[l].rearrange("(kc k) n -> k kc n", k=128), writes=["wqb"])
            S.dma(POOL, wkk[:, :], wkk_d[l], writes=["wkk"])
            S.dma(POOL, wkv[:, :], wkv_d[l], writes=["wkv"])

        def norm_stats(tb, nb):
            (rstd, k_rstd), (lnt, k_lnt), sqs = nb
            pn = 6 + (tb % 2)
            for c in range(8):
                sq, ksq = sqs[c % 3]
                AC(sq, XT(c, tb * 512, 512), AF.Square, [("xT", c, tb)], [ksq])
                MM(ps[pn][:, :], ones_b, sq, c == 0, c == 7, ["cb", ksq], PK(pn))
            r_ = rstd[:, tb * 512:(tb + 1) * 512]
            AC(lnt, ps[pn][:, :], AF.Ln, [], [PK(pn), k_lnt], scale=1.0 / D_MODEL, bias=1e-6)
            AC(r_, lnt, AF.Exp, [k_lnt], [k_rstd], scale=-0.5)

        def norm_apply(tb, nb, gain_ap, final):
            (rstd, k_rstd), (lnt, k_lnt), sqs = nb
            r_ = rstd[:, tb * 512:(tb + 1) * 512]
            for c in range(8):
                if final:
                    STT(DVE, XT(c, tb * 512, 512), XT(c, tb * 512, 512), gain_ap(c), r_, ALU.mult, ALU.mult,
                        [("xT", c, tb), k_rstd, "fin"], [("xT", c, tb)])
                    S.dma(SP, out_d[c * 128:(c + 1) * 128, tb * 512:(tb + 1) * 512], XT(c, tb * 512, 512), reads=[("xT", c, tb)])
                else:
                    STT(DVE, XN(tb, c), XT(c, tb * 512, 512), gain_ap(c), r_, ALU.mult, ALU.mult,
                        [("xT", c, tb), k_rstd, "vec"], [("xn", tb)])

        def norm_tb(tb, nb, gain_ap, final):
            norm_stats(tb, nb)
            norm_apply(tb, nb, gain_ap, final)

        def norm_alloc(tag):
            return (SC.f32("rstd" + tag, S_LEN), SC.f32("lnt" + tag, 512), [SC.bf16("sq%s%d" % (tag, i), 512) for i in range(3)])

        sA = None
        sB2_pre = None
        for l in range(depth):
            vb = l * NV
            VC = lambda i: vec[:, vb + i: vb + i + 1]
            s0, s1, s2 = (l % 3), ((l + 1) % 3), ((l + 2) % 3)
            if l == 0:
                S.mark('L%d_P0' % l)
                SC.reset()
                nb0 = norm_alloc("P")
                gate_x = [("xT", c_, 2) for c_ in range(8)]
                sA = [load_ring(win_d[l][:, 0:512], 512, s0), load_ring(win_d[l][:, 512:1024], 512, s1, gate_x)]
                sB2_pre = load_ring(win_d[l][:, 2048:2310], 262, s2, gate_x)
                load_small(l)
                for tb in range(NTB):
                    norm_tb(tb, nb0, lambda c: vec[:, V_NG + c: V_NG + c + 1], False)

            S.mark('L%d_A' % l)
            SC.reset()
            ubuf, k_u = SC.f32("u", 2 + S_LEN)
            tA = [[SC.f32("tA%d_%d" % (i, b), 512) for i in range(4)] for b in range(2)]
            it = 0
            for j in range(2):
                s = sA[j]
                MS(DVE, ubuf[:, 0:2], 0.0, [k_u])
                for tb in range(NTB):
                    (t_sz, k_sz), (t_h, k_h), (t_g, k_g), (t_v, k_v) = tA[it % 2]
                    pb = (it % 2) * 4
                    it += 1
                    for u in range(4):
                        proj(pb + u if pb + u < 7 else 3, s, u * 128, 128, tb)
                    p_h, p_b, p_c, p_z = [pb + u if pb + u < 7 else 3 for u in range(4)]
                    AC(t_sz, ps[p_z][:, :], AF.Silu, [], [PK(p_z), k_sz])
                    AC(t_h, ps[p_h][:, :], AF.Copy, [], [PK(p_h), k_h])
                    o = 2 + tb * 512
                    TT(DVE, ubuf[:, o:o + 512], ps[p_c][:, :], t_h, ALU.mult, [k_h], [PK(p_c), k_u])
                    TT(DVE, t_g, ps[p_b][:, :], t_sz, ALU.mult, [k_sz], [PK(p_b), k_g])
                    TS(DVE, t_v, ubuf[:, o:o + 512], VC(V_CA + j * 3 + 2), None, ALU.mult, None, [k_u, "vec"], [k_v])
                    STT(DVE, t_v, ubuf[:, o - 1:o + 511], VC(V_CA + j * 3 + 1), t_v, ALU.mult, ALU.add, [k_u, k_v, "vec"], [k_v])
                    STT(DVE, t_v, ubuf[:, o - 2:o + 510], VC(V_CA + j * 3 + 0), t_v, ALU.mult, ALU.add, [k_u, k_v, "vec"], [k_v])
                    TT(DVE, YT(j, tb * 512, 512), t_v, t_g, ALU.mult, [k_v, k_g], [("yT", j)])
                if j == 0:
                    sB0_pre = load_ring(win_d[l][:, 1024:1536], 512, s0)
            if stop_after == "A":
                break

            S.mark('L%d_B' % l)
            SC.reset()
            sB = [sB0_pre, load_ring(win_d[l][:, 1536:2048], 512, s1), sB2_pre]
            ccmap = [(sB[0], 0), (sB[0], 128), (sB[0], 256), (sB[0], 384), (sB[1], 0), (sB[1], 128), (sB[1], 256)]
            zmap = [(sB[1], 384), (sB[2], 0), (sB[2], 128)]
            raws = [SC.f32("raw%d" % i, 515) for i in range(2)]
            yvx, k_yvx = SC.f32("yvx", 504)
            yv = scr[:, 0:1536]
            yvk = [raws[0][1], raws[1][1], k_yvx]
            halo, k_halo = SC.f32("halo", 24)
            xsT, k_xs = SC.f32("xsT", 3 * 512)
            dtts = [SC.f32("dtt%d" % i, 24) for i in range(2)]
            lats = [SC.f32("lat%d" % i, 24) for i in range(2)]
            cscs = [SC.f32("csc%d" % i, 24) for i in range(2)]
            Abr, k_A = SC.f32("Abr", 6)
            Rf, k_Rf = SC.f32("Rf", 768)
            Ebs = [SC.bf16("Eb%d" % i, 768) for i in range(2)]
            dLs = [SC.f32("dL%d" % i, 768) for i in range(2)]
            acc, k_acc = dLs[1][0][:, 0:512], dLs[1][1]
            tsz, k_tsz = dLs[0][0][:, 0:512], dLs[0][1]
            dsas = [SC.f32("dsa%d" % i, 6) for i in range(2)]
            Sf, k_Sf = SC.f32("Sf", 384)
            BT, k_BT = SC.bf16("BT", 1024)
            CT, k_CT = SC.bf16("CT", 1024)
            off_MT = SC.off
            MTs = [SC.bf16("MT%d" % i, 768) for i in range(2)]
            off_CdT = SC.off
            CdTs = [SC.bf16("CdT%d" % i, 768) for i in range(2)]
            accC, kaccC = scr[:, off_MT:off_MT + 512], [MTs[0][1], MTs[1][1]]
            tszC, ktszC = scr[:, off_CdT:off_CdT + 512], [CdTs[0][1], CdTs[1][1]]
            xds = [SC.bf16("xd%d" % i, 384) for i in range(2)]
            xdds = [SC.bf16("xdd%d" % i, 384) for i in range(2)]
            Btoks = [SC.bf16("Btok%d" % i, 256) for i in range(2)]
            cdbs = [SC.f32("cdb%d" % i, 6) for i in range(4)]
            Sb, k_Sb = SC.bf16("Sb", 384)
            sqB = [(Ebs[0][0][:, 0:512], Ebs[0][1]), (Ebs[1][0][:, 0:512], Ebs[1][1]), (Rf.bitcast(BF16)[:, 0:512], k_Rf)]
            gt = [(tsz, k_tsz), (acc, k_acc), (Rf[:, 256:768], k_Rf)]
            zbank = (5, 1, 7)
            plan = {0: [(0, ones_b), (1, gsel[0])], 1: [(0, gsel[1]), (1, gsel[2]), (2, gsel[3])],
                    2: [(1, gsel[4]), (2, ones_b)]}

            def gate_A(tbx):
                for j in range(3):
                    pz = zbank[j]
                    g_, kg_ = gt[j]
                    AC(g_, ps[pz][:, :], AF.Silu, [], [PK(pz), kg_])
                for j in range(3):
                    g_, kg_ = gt[j]
                    xs_j = xsT[:, j * 512:(j + 1) * 512]
                    STT(DVE, xs_j, xs_j, VC(V_SD + j), yv[:, j * 512:(j + 1) * 512], ALU.mult, ALU.add, [k_xs, "vec"] + yvk, [k_xs])
                    TT(DVE, xs_j, xs_j, g_, ALU.mult, [k_xs, kg_], [k_xs])
                    AC(sqB[j][0], xs_j, AF.Square, [k_xs], [sqB[j][1]])

            def gate_B1(tbx):
                for oc in range(3):
                    pn = 3 + oc
                    lst = plan[oc]
                    for i, (kc, g_ap) in enumerate(lst):
                        MM(ps[pn][:, :], g_ap, sqB[kc][0], i == 0, i == len(lst) - 1, ["cb", sqB[kc][1]], PK(pn))
                for oc in range(3):
                    pn = 3 + oc
                    g_, kg_ = gt[oc]
                    AC(g_, ps[pn][:, :], AF.Ln, [], [PK(pn), kg_], scale=1.0 / 192.0, bias=1e-5)
                    AC(g_, g_, AF.Exp, [kg_], [kg_], scale=-0.5)

            def gate_B2(tbx):
                for oc in range(3):
                    g_, kg_ = gt[oc]
                    STT(DVE, YT(2 + oc, tbx * 512, 512), xsT[:, oc * 512:(oc + 1) * 512], VC(V_SNG + oc), g_,
                        ALU.mult, ALU.mult, [k_xs, kg_, "vec"], [("yT", 2 + oc)])

            rb = l * 12
            AC(Abr, rowb[:, rb + 6:rb + 12], AF.Exp, ["rowb"], [k_A])
            TS(DVE, Abr, Abr, -1.0, None, ALU.mult, None, [k_A], [k_A])
            MS(DVE, Sf, 0.0, [k_Sf])
            MS(DVE, Sb, 0.0, [k_Sb])
            MS(DVE, halo, 0.0, [k_halo])
            ri = 0
            def dt_stage(tb):
                (dtt, k_dt), (lat, k_la), (csc, k_csc) = dtts[tb % 2], lats[tb % 2], cscs[tb % 2]
                for c in range(4):
                    for kc in range(8):
                        MM(ps[2][:, c * 8:c * 8 + 6], XN(tb, kc, c * 128, 128), W(sB[2], kc, 256, 6), kc == 0, kc == 7,
                           [("xn", tb), ("ring", sB[2])], PK(2))
                dt3 = dtt.rearrange("p (c h) -> p c h", c=4)
                TT(DVE, dt3, ps[2][:, 0:32].rearrange("p (c e) -> p c e", c=4)[:, :, 0:6],
                   rowb[:, rb:rb + 6].unsqueeze(1).to_broadcast([128, 4, 6]), ALU.add, ["rowb"], [PK(2), k_dt])
                AC(dtt, dtt, AF.Exp, [k_dt], [k_dt])
                AC(dtt, dtt, AF.Ln, [k_dt], [k_dt], bias=1.0)
                TT(DVE, lat.rearrange("p (c h) -> p c h", c=4), dt3, Abr.unsqueeze(1).to_broadcast([128, 4, 6]),
                   ALU.mult, [k_dt, k_A], [k_la])
                MM(ps[2][:, 64:88], tri_f, lat, True, True, ["cf", k_la], PK(2))
                CP(DVE, csc, ps[2][:, 64:88], [], [PK(2), k_csc])

            dt_stage(0)
            for tb in range(NTB):
                (dtt, k_dt), (lat, k_la), (csc, k_csc) = dtts[tb % 2], lats[tb % 2], cscs[tb % 2]
                S.mark('L%d_B%d_dt' % (l, tb))
                def decay(c):
                    gi = tb * 4 + c
                    (Eb, k_Eb), (dL, k_dL), (dsa, k_dsa) = Ebs[gi % 2], dLs[gi % 2], dsas[gi % 2]
                    cdb, k_cdb = cdbs[gi % 4]
                    la_c = lat[:, c * 6:(c + 1) * 6]
                    cs_c = csc[:, c * 6:(c + 1) * 6]
                    dL3 = dL.rearrange("p (h l) -> p h l", h=6)
                    TT(POOL, Rf.rearrange("p (h l) -> p h l", h=6), la_c.unsqueeze(2).to_broadcast([128, 6, 128]),
                       tri_f.unsqueeze(1).to_broadcast([128, 6, 128]), ALU.mult, [k_la, "cf"], [k_Rf])
                    MM(ps[3][:, 0:384], ones_f, Rf[:, 0:384], True, True, ["cf", k_Rf], PK(3))
                    MM(ps[4][:, 0:384], ones_f, Rf[:, 384:768], True, True, ["cf", k_Rf], PK(4))
                    for hb in range(2):
                        AC(Eb[:, hb * 384:(hb + 1) * 384], ps[3 + hb][:, 0:384], AF.Exp, [], [PK(3 + hb), k_Eb])
                    for hb in range(2):
                        for r in range(3):
                            MM(ps[3 + hb][:, r * 128:(r + 1) * 128], ident_f, negm_f, False, True, ["cf"], PK(3 + hb), sgc=True)
                    for hb in range(2):
                        pq = ps[3 + hb][:, 0:384].rearrange("p (h l) -> p h l", h=3)
                        TT(DVE, dL3[:, hb * 3:(hb + 1) * 3, :], pq,
                           cs_c[:, hb * 3:(hb + 1) * 3].unsqueeze(2).to_broadcast([128, 3, 128]), ALU.subtract,
                           [k_csc], [PK(3 + hb), k_dL])
                        TT(DVE, dsa[:, hb * 3:(hb + 1) * 3], pq[:, :, 127], cs_c[:, hb * 3:(hb + 1) * 3], ALU.subtract,
                           [k_csc], [PK(3 + hb), k_dsa])
                        AC(cdb[:, hb * 3:(hb + 1) * 3], pq[:, :, 127], AF.Exp, [], [PK(3 + hb), k_cdb])
                    AC(dL, dL, AF.Exp, [k_dL], [k_dL])
                    AC(dsa, dsa, AF.Exp, [k_dsa], [k_dsa])

                S.mark('L%d_B%d_conv' % (l, tb))
                cbank = (0, 1, 5, 6)

                corder = (3, 4, 5, 6, 0, 1, 2)

                def conv_proj(p):
                    sl, c0 = ccmap[corder[p]]
                    proj(cbank[p % 4], sl, c0, 128, tb)

                def conv_evac(p):
                    cc = corder[p]
                    raw, k_raw = raws[p % 2]
                    pidx = cbank[p % 4]
                    CP(POOL, raw[:, 0:3], halo[:, cc * 3:cc * 3 + 3], [k_halo], [k_raw])
                    AC(raw[:, 3:515], ps[pidx][:, :], AF.Copy, [], [PK(pidx), k_raw])
                    ac_, k_ac = (accC, kaccC) if p % 2 == 0 else (tszC, ktszC)
                    AC(ac_, ps[pidx][:, :], AF.Copy, ["vec"], [PK(pidx)] + k_ac, scale=VC(V_SCW + cc * 4 + 3))

                def conv_back(p):
                    cc = corder[p]
                    raw, k_raw = raws[p % 2]
                    ac_, k_ac = (accC, kaccC) if p % 2 == 0 else (tszC, ktszC)
                    wv = lambda k: VC(V_SCW + cc * 4 + k)
                    CP(POOL, halo[:, cc * 3:cc * 3 + 3], raw[:, 512:515], [k_raw], [k_halo])
                    STT(DVE, ac_, raw[:, 2:514], wv(2), ac_, ALU.mult, ALU.add, [k_raw, "vec"] + k_ac, k_ac)
                    STT(DVE, ac_, raw[:, 1:513], wv(1), ac_, ALU.mult, ALU.add, [k_raw, "vec"] + k_ac, k_ac)
                    STT(DVE, ac_, raw[:, 0:512], wv(0), ac_, ALU.mult, ALU.add, [k_raw, "vec"] + k_ac, k_ac)
                    if cc < 3:
                        dst, kd = xsT[:, cc * 512:(cc + 1) * 512], k_xs
                    elif cc < 5:
                        dst, kd = BT[:, (cc - 3) * 512:(cc - 2) * 512], k_BT
                    else:
                        dst, kd = CT[:, (cc - 5) * 512:(cc - 4) * 512], k_CT
                    AC(dst, ac_, AF.Silu, ["vec"] + k_ac, [kd], bias=VC(V_SCB + cc))

                if tb > 0:
                    S.mark('L%d_B%d_gate' % (l, tb - 1))
                    gate_A(tb - 1)
                conv_proj(0)
                conv_proj(1)
                conv_evac(0)
                if tb > 0:
                    gate_B1(tb - 1)
                for p in range(7):
                    if p + 2 < 7:
                        conv_proj(p + 2)
                    if p + 1 < 7:
                        conv_evac(p + 1)
                    conv_back(p)
                    if p == 0 and tb > 0:
                        gate_B2(tb - 1)
                decay(0)
                decay(1)
                if tb == NTB - 1:
                    sC0_pre = load_ring(win_d[l][:, 2310:2758], 448, s0)
                S.mark('L%d_B%d_chunks' % (l, tb))
                def front(c):
                    gi = tb * 4 + c
                    t0 = c * 128
                    (Eb, k_Eb), (dL, k_dL), (dsa, k_dsa) = Ebs[gi % 2], dLs[gi % 2], dsas[gi % 2]
                    (MT, k_MT), (CdT, k_CdT), (xd, k_xd), (xdd, k_xdd) = MTs[gi % 2], CdTs[gi % 2], xds[gi % 2], xdds[gi % 2]
                    pT = 5 if gi % 2 == 0 else 2
                    bo = (gi % 2) * 256
                    for g in range(2):
                        MM(ps[1][:, g * 128:(g + 1) * 128], BT[:, g * 512 + t0:g * 512 + t0 + 128],
                           CT[:, g * 512 + t0:g * 512 + t0 + 128], True, True, [k_BT, k_CT], PK(1))
                    for j in range(3):
                        TR(ps[pT][:, j * 128:(j + 1) * 128], xsT[:, j * 512 + t0:j * 512 + t0 + 128], ident_f, [k_xs, "cf"], PK(pT))
                    for g in range(2):
                        TR(psb[:, bo + g * 128:bo + (g + 1) * 128], BT[:, g * 512 + t0:g * 512 + t0 + 128], ident_b, [k_BT, "cb"], PK(7))
                    TT(DVE, MT.rearrange("p (g r l) -> p g r l", g=2, r=3),
                       ps[1][:, 0:256].rearrange("p (g l) -> p g l", g=2).unsqueeze(2).to_broadcast([128, 2, 3, 128]),
                       dL.rearrange("p (g r l) -> p g r l", g=2, r=3), ALU.mult, [k_dL], [PK(1), k_MT])
                    TT(DVE, xd.rearrange("p (h q) -> p h q", h=6), ps[pT][:, 0:384].rearrange("p (h q) -> p h q", h=6),
                       dtt[:, c * 6:(c + 1) * 6].unsqueeze(2).to_broadcast([128, 6, 64]), ALU.mult, [k_dt], [PK(pT), k_xd])
                    TT(DVE, CdT.rearrange("p (g r l) -> p g r l", g=2, r=3),
                       CT.rearrange("p (g t) -> p g t", g=2)[:, :, t0:t0 + 128].unsqueeze(2).to_broadcast([128, 2, 3, 128]),
                       Eb.rearrange("p (g r l) -> p g r l", g=2, r=3), ALU.mult, [k_CT, k_Eb], [k_CdT])
                    TT(DVE, xdd.rearrange("p (h q) -> p h q", h=6), xd.rearrange("p (h q) -> p h q", h=6),
                       dsa.unsqueeze(2).to_broadcast([128, 6, 64]), ALU.mult, [k_xd, k_dsa], [k_xdd])
                    Bt, k_Bt = Btoks[gi % 2]
                    CP(ACT, Bt, psb[:, bo:bo + 256], [], [PK(7), k_Bt])

                def back(c):
                    gi = tb * 4 + c
                    t0 = c * 128
                    (MT, k_MT), (CdT, k_CdT), (xd, k_xd), (xdd, k_xdd) = MTs[gi % 2], CdTs[gi % 2], xds[gi % 2], xdds[gi % 2]
                    cdb, k_cdb = cdbs[gi % 4]
                    Bt, k_Bt = Btoks[gi % 2]
                    for g in range(2):
                        MM(ps[0][:, g * 192:(g + 1) * 192], Bt[:, g * 128:(g + 1) * 128], xdd[:, g * 192:(g + 1) * 192],
                           True, True, [k_Bt, k_xdd], PK(0))
                    for h in range(6):
                        o_ap = ps[6][(h % 2) * 64:(h % 2) * 64 + 64, (h // 2) * 128:(h // 2) * 128 + 128]
                        MM(o_ap, xd[:, h * 64:(h + 1) * 64], MT[:, h * 128:(h + 1) * 128], True, False, [k_xd, k_MT], PK(6))
                        MM(o_ap, Sb[:, h * 64:(h + 1) * 64], CdT[:, h * 128:(h + 1) * 128], False, True, [k_Sb, k_CdT], PK(6))
                    Sf3 = Sf.rearrange("p (h q) -> p h q", h=6)
                    TT(DVE, Sf3, Sf3, cdb.unsqueeze(2).to_broadcast([128, 6, 64]), ALU.mult, [k_Sf, k_cdb], [k_Sf])
                    TT(DVE, Sf, Sf, ps[0][:, 0:384], ALU.add, [k_Sf], [PK(0), k_Sf])
                    CP(DVE, Sb, Sf, [k_Sf], [k_Sb])
                    CP(ACT, yv.rearrange("p (j t) -> p j t", j=3)[:, :, t0:t0 + 128],
                       ps[6][:, 0:384].rearrange("p (j l) -> p j l", j=3), [], [PK(6)] + yvk)

                front(0)
                for c in range(4):
                    if c + 2 < 4:
                        decay(c + 2)
                    if c + 1 < 4:
                        front(c + 1)
                    if c == 2:
                        for j in range(3):
                            sl, c0 = zmap[j]
                            proj(zbank[j], sl, c0, 128, tb)
                    back(c)
                if tb + 1 < NTB:
                    dt_stage(tb + 1)
            S.mark('L%d_Bgate_last' % l)
            gate_A(NTB - 1)
            gate_B1(NTB - 1)
            gate_B2(NTB - 1)
            sC = [sC0_pre, load_ring(win_d[l][:, 2758:3142], 384, s1)]
            sD = [load_ring(wout_d[l][:, 0:512], 512, s2), None]
            if stop_after == "B":
                break

            S.mark('L%d_C1' % l)
            SC.reset()
            rope, k_rope = SC.f32("rope", S_LEN)
            qa_nT, k_qan = SC.bf16("qa_nT", 2 * S_LEN)
            kv_nT, k_kvn = SC.bf16("kv_nT", S_LEN)
            KT, k_KT = SC.bf16("KT", S_LEN)
            Vaug, k_V = SC.bf16("Vaug", 16 * 128)
            off_c1 = SC.off
            qa_raw, k_qar = SC.f32("qa_raw", 1024)
            kv_raw, k_kvr = SC.f32("kv_raw", 512)
            tmpc, k_tmpc = SC.f32("tmpc", 512)
            tmpd, k_tmpd = SC.f32("tmpd", 512)
            sqC = [SC.bf16("sqC%d" % i, 512) for i in range(3)]
            tmpe, k_tmpe = SC.f32("tmpe", 512)
            tmpg, k_tmpg = SC.f32("tmpg", 512)
            def rope_build(tb):
                S.dma(SP, tmpe[0:64, :], posf_d[:, tb * 512:(tb + 1) * 512].partition_broadcast(64), reads=["posf"], writes=[k_tmpe])
                TS(DVE, tmpe[0:64, :], tmpe[0:64, :], cf[0:64, 512:513], cf[0:64, 513:514], ALU.mult, ALU.add, [k_tmpe, "cf"], [k_tmpe])
                rr = rope[0:64, tb * 512:(tb + 1) * 512]
                TS(DVE, rr, tmpe[0:64, :], 1.0 / TWO_PI, None, ALU.mult, None, [k_tmpe], [k_rope])
                for q in range(4):
                    CP(DVE, posi[0:64, :], rr[:, q * 128:(q + 1) * 128], [k_rope], ["posi"])
                    CP(DVE, rr[:, q * 128:(q + 1) * 128], posi[0:64, :], ["posi"], [k_rope])
                STT(DVE, tmpe[0:64, :], rr, -TWO_PI, tmpe[0:64, :], ALU.mult, ALU.add, [k_rope, k_tmpe], [k_tmpe])
                TS(DVE, rr, tmpe[0:64, :], math.pi, -TWO_PI, ALU.is_gt, ALU.mult, [k_tmpe], [k_rope])
                TT(DVE, rr, tmpe[0:64, :], rr, ALU.add, [k_tmpe, k_rope], [k_rope])
            MS(DVE, KT[32:64, :], 0.0, [k_KT])
            MS(DVE, Vaug.rearrange("p (t e) -> p t e", t=16)[:, :, 64:128], 1.0, [k_V])
            szk = []
            rope_build(0)
            for tb in range(NTB):
                qb_ = (0, 1) if tb % 2 == 0 else (6, 7)
                proj(qb_[0], sC[0], 0, 128, tb)
                proj(qb_[1], sC[0], 128, 128, tb)
                proj(3, sC[0], 256, 128, tb)
                for kc in range(2):
                    AC(qa_raw[:, kc * 512:(kc + 1) * 512], ps[qb_[kc]][:, :], AF.Copy, [], [PK(qb_[kc]), k_qar])
                    AC(sqC[kc][0], qa_raw[:, kc * 512:(kc + 1) * 512], AF.Square, [k_qar], [sqC[kc][1]])
                AC(kv_raw, ps[3][:, :], AF.Copy, [], [PK(3), k_kvr])
                AC(sqC[2][0], kv_raw, AF.Square, [k_kvr], [sqC[2][1]])
                for kc in range(2):
                    MM(ps[2][:, :], ones_b, sqC[kc][0], kc == 0, kc == 1, ["cb", sqC[kc][1]], PK(2))
                MM(ps[4][:, :], ones_b, sqC[2][0], True, True, ["cb", sqC[2][1]], PK(4))
                AC(tmpc, ps[2][:, :], AF.Ln, [], [PK(2), k_tmpc], scale=1.0 / 256.0, bias=1e-6)
                AC(tmpc, tmpc, AF.Exp, [k_tmpc], [k_tmpc], scale=-0.5)
                AC(tmpd, ps[4][:, :], AF.Ln, [], [PK(4), k_tmpd], scale=1.0 / 128.0, bias=1e-6)
                AC(tmpd, tmpd, AF.Exp, [k_tmpd], [k_tmpd], scale=-0.5)
                for kc in range(2):
                    STT(DVE, qa_nT[:, kc * S_LEN + tb * 512: kc * S_LEN + (tb + 1) * 512], qa_raw[:, kc * 512:(kc + 1) * 512],
                        VC(V_QNG + kc), tmpc, ALU.mult, ALU.mult, [k_qar, k_tmpc, "vec"], [k_qan])
                STT(DVE, kv_nT[:, tb * 512:(tb + 1) * 512], kv_raw, VC(V_KNG), tmpd, ALU.mult, ALU.mult,
                    [k_kvr, k_tmpd, "vec"], [k_kvn])
                if tb + 1 < NTB:
                    rope_build(tb + 1)
            for tb in range(NTB):
                rr = rope[0:64, tb * 512:(tb + 1) * 512]
                AC(rr, rr, AF.Sin, [k_rope], [k_rope])
            for tb in range(NTB):
                r_tb = rope[:, tb * 512:(tb + 1) * 512]
                pk3 = 5 if tb % 2 == 0 else 4
                proj(pk3, sC[0], 384, 64, tb)
                TT(DVE, tmpg[0:32, :], ps[pk3][0:32, :], r_tb[0:32, :], ALU.mult, [k_rope], [PK(pk3), k_tmpg])
                TT(DVE, tmpe[0:32, :], ps[pk3][32:64, :], r_tb[32:64, :], ALU.mult, [k_rope], [PK(pk3), k_tmpe])
                TT(DVE, KT[0:32, tb * 512:(tb + 1) * 512], tmpg[0:32, :], tmpe[0:32, :], ALU.add, [k_tmpg, k_tmpe], [k_KT])
            sD[1] = load_ring(wout_d[l][:, 512:1024], 512, s0)
            czb = (6, 7, 0, 1, 2, 3)
            for tb in range(NTB):
                for j in range(3):
                    proj(czb[(3 * tb + j) % 6], sC[1], j * 128, 128, tb)
                kz = ("sz", l, tb)
                S.inherit(kz, [("xn", tb)])
                szk.append(kz)
                szv = xn[:, tb * 4096: tb * 4096 + 3072].bitcast(F32)
                for j in range(3):
                    pz = czb[(3 * tb + j) % 6]
                    AC(szv[:, j * 512:(j + 1) * 512], ps[pz][:, :], AF.Silu, [], [PK(pz), kz])
            S.mark('L%d_C2' % l)
            SC.off = off_c1
            c1keys = [k_qar, k_kvr, k_tmpc, k_tmpd, k_tmpe, k_tmpg, sqC[0][1], sqC[1][1], sqC[2][1]]
            tq, k_tq = SC.f32("tq", 512)
            tr_, k_tr = SC.f32("tr", 512)
            rec, k_rec = SC.f32("rec", 512)
            for k_ in (k_tq, k_tr, k_rec):
                S.inherit(k_, c1keys)
            QTb, PTb = [], []
            for i in range(2):
                a = xn[:, i * 4096 + 3072: i * 4096 + 3584]
                k = ("QT", l, i)
                S.inherit(k, [("xn", i)])
                QTb.append((a, k))
                MS(DVE, a[32:64, :], 0.0, [k])
            for i in range(4):
                tbr = 2 + i // 2
                o = tbr * 4096 + 3072 + (i % 2) * 512
                k = ("PT", l, i)
                S.inherit(k, [("xn", tbr)])
                PTb.append((xn[:, o:o + 512], k))
            scale = (64 + 32) ** -0.5
            rs = sC[1]
            KT2 = ring[rs][:, 0:S_LEN]
            V2 = ring[rs][:, S_LEN:2 * S_LEN]
            k_KT2, k_V2 = ("KT2", l), ("V2", l)
            S.inherit(k_KT2, [("ring", rs)])
            S.inherit(k_V2, [("ring", rs)])
            CP(DVE, KT2[0:32, :], KT[0:32, :], [k_KT], [k_KT2])
            MS(DVE, KT2[32:64, :], 0.0, [k_KT2])
            MS(DVE, V2.rearrange("p (t e) -> p t e", t=16)[:, :, 64:128], 1.0, [k_V2])
            KTs = [(KT, k_KT), (KT2, k_KT2)]
            Vs = [(Vaug, k_V), (V2, k_V2)]
            blocks = [(h, qb) for h in range(6) for qb in range(NTB)]
            NKE = 5
            early = [(dc_, tb_, kc_) for half_ in range(2) for tb_ in range(NTB) for dc_ in range(4 * half_, 4 * half_ + 4) for kc_ in range(NKE)]
            early_i = [0]

            hb_rr = [0]

            def prep_head_piece(h, i):
                KTh, kK = KTs[h % 2]
                Vh, kV = Vs[h % 2]
                pk = (1, 2)[hb_rr[0] % 2]
                hb_rr[0] += 1
                if i < 4:
                    tb = i
                    MM(ps[pk][:, :], wkk[:, h * 128:(h + 1) * 128], kv_nT[:, tb * 512:(tb + 1) * 512], True, True,
                       ["wkk", k_kvn], PK(pk))
                    CP(DVE, KTh[64:128, tb * 512:(tb + 1) * 512], ps[pk][64:128, :], [], [PK(pk), kK])
                else:
                    half = i - 4
                    for t in range(8):
                        kt = half * 8 + t
                        MM(ps[pk][:, t * 64:(t + 1) * 64], kv_nT[:, kt * 128:(kt + 1) * 128], wkv[:, h * 64:(h + 1) * 64],
                           True, True, [k_kvn, "wkv"], PK(pk))
                    CP(DVE, Vh.rearrange("p (t e) -> p t e", t=16)[:, half * 8:(half + 1) * 8, 0:64],
                       ps[pk][:, :].rearrange("p (t e) -> p t e", t=8), [], [PK(pk), kV])

            def prep_head(h):
                for i in range(6):
                    prep_head_piece(h, i)

            def prep_q_pe(bi):
                h, qb = blocks[bi]
                for kc in range(2):
                    MM(ps[2][:, :], wqb[:, kc * 768 + h * 128: kc * 768 + (h + 1) * 128],
                       qa_nT[:, kc * S_LEN + qb * 512: kc * S_LEN + (qb + 1) * 512], kc == 0, kc == 1, ["wqb", k_qan], PK(2))

            def prep_q_ev(bi):
                h, qb = blocks[bi]
                QT, k_QT = QTb[bi % 2]
                r_qb = rope[:, qb * 512:(qb + 1) * 512]
                CP(DVE, QT[64:128, :], ps[2][64:128, :], [], [PK(2), k_QT])
                TT(DVE, tq[0:32, :], ps[2][0:32, :], r_qb[0:32, :], ALU.mult, [k_rope], [PK(2), k_tq])
                TT(DVE, tr_[0:32, :], ps[2][32:64, :], r_qb[32:64, :], ALU.mult, [k_rope], [PK(2), k_tr])
                TT(DVE, QT[0:32, :], tq[0:32, :], tr_[0:32, :], ALU.add, [k_tq, k_tr], [k_QT])

            def prep_q(bi):
                prep_q_pe(bi)
                prep_q_ev(bi)

            G = []
            for bi, (h, qb) in enumerate(blocks):
                for kt in range(4 * qb + 4):
                    G.append((bi, kt))

            def qk(g):
                bi, kt = G[g]
                h, qb = blocks[bi]
                QT, k_QT = QTb[bi % 2]
                KTh, kK = KTs[h % 2]
                j = kt - 4 * qb
                q0 = max(0, j) * 128
                pS = 3 + (g % 2)
                MM(ps[pS][:, q0:512], KTh[:, kt * 128:(kt + 1) * 128], QT[:, q0:512], True, j < 0, [kK, k_QT], PK(pS))
                if j >= 0:
                    MM(ps[pS][:, q0:q0 + 128], ident_b, negm_b, False, True, ["cb"], PK(pS))
                for _ in range(N_FILL):
                    if early_i[0] < len(early):
                        dc_, tb_, kc_ = early[early_i[0]]
                        bk_ = (0, 7)[(early_i[0] // NKE) % 2]
                        early_i[0] += 1
                        s_ = sD[dc_ // 4]
                        MM(ps[bk_][:, :], W(s_, kc_, (dc_ % 4) * 128, 128), YT(kc_, tb_ * 512, 512), kc_ == 0, kc_ == NKE - 1,
                           [("ring", s_), ("yT", kc_)], PK(bk_))
                        if kc_ == NKE - 1:
                            TT(DVE, XT(dc_, tb_ * 512, 512), XT(dc_, tb_ * 512, 512), ps[bk_][:, :], ALU.add,
                               [("xT", dc_, tb_)], [PK(bk_), ("xT", dc_, tb_)])
                    else:
                        MM(ps[7][:, :], ident_b, qa_nT[:, 0:512], True, True, ["cb", k_qan], PK(7))

            def pv(g):
                bi, kt = G[g]
                h, qb = blocks[bi]
                Vh, kV = Vs[h % 2]
                nk = 4 * qb + 4
                j = kt - 4 * qb
                q0 = max(0, j) * 128
                pS = 3 + (g % 2)
                po = 5 + (bi % 2)
                PT, k_PT = PTb[g % 4]
                AC(PT[:, q0:512], ps[pS][:, q0:512], AF.Exp, [], [PK(pS), k_PT], scale=scale)
                MM(ps[po][:, q0:512], Vh[:, kt * 128:(kt + 1) * 128], PT[:, q0:512], kt == 0, kt == nk - 1,
                   [kV, k_PT], PK(po))
                if kt == 1 and pend_fin:
                    pend_fin.pop(0)()
                if kt == nk - 1:
                    def fin(h=h, qb=qb, po=po):
                        AC(rec[0:64, :], ps[po][64:128, :], AF.Ln, [], [PK(po), k_rec])
                        AC(rec[0:64, :], rec[0:64, :], AF.Exp, [k_rec], [k_rec], scale=-1.0)
                        hp, hr = h // 2, (h % 2) * 64
                        TT(DVE, tq[hr:hr + 64, :], ps[po][0:64, :], rec[0:64, :], ALU.mult, [k_rec], [PK(po), k_tq])
                        szv = xn[:, qb * 4096: qb * 4096 + 3072].bitcast(F32)
                        TT(DVE, YT(5 + hp, qb * 512, 512)[hr:hr + 64, :], tq[hr:hr + 64, :], szv[hr:hr + 64, hp * 512:(hp + 1) * 512],
                           ALU.mult, [k_tq, szk[qb]], [("yT", 5 + hp)])
                    pend_fin.append(fin)

            pend_fin = []
            prep_head(0)
            prep_q(0)
            for g in range(len(G) + 1):
                if g < len(G):
                    bi, kt = G[g]
                    h, qb = blocks[bi]
                    if kt == 0 and bi + 1 < len(blocks):
                        prep_q_pe(bi + 1)
                    if kt == 1 and bi + 1 < len(blocks):
                        prep_q_ev(bi + 1)
                    if h + 1 < 6:
                        if qb == 1 and kt in (3, 5, 7):
                            prep_head_piece(h + 1, (kt - 3) // 2)
                        if qb == 2 and kt in (3, 5, 7):
                            prep_head_piece(h + 1, 3 + (kt - 3) // 2)
                    qk(g)
                if g >= 1:
                    pv(g - 1)
            while pend_fin:
                pend_fin.pop(0)()
            S.add_frontier(("ring", rs), [k_KT2, k_V2])
            for i in range(2):
                S.add_frontier(("xn", i), [szk[i], QTb[i][1]])
            for i in range(4):
                S.add_frontier(("xn", 2 + i // 2), [szk[2 + i // 2], PTb[i][1]])
            if stop_after == "C":
                break

            S.mark('L%d_D' % l)
            SC.reset()
            nbn = norm_alloc("N%d" % l)
            last = (l == DEPTH - 1)
            if not last and l + 1 < depth:
                sA = [load_ring(win_d[l + 1][:, 0:512], 512, s1), None]
                load_small(l + 1)
            pi = 0
            pend_apply = []
            for tb in range(NTB):
                for dc in range(8):
                    s_ = sD[dc // 4]
                    c0 = (dc % 4) * 128
                    pp = pi % 6
                    pi += 1
                    kcs = list(range(NKE, 8)) if early_i[0] >= len(early) else list(range(8))
                    for kc in kcs:
                        MM(ps[pp][:, :], W(s_, kc, c0, 128), YT(kc, tb * 512, 512), kc == kcs[0], kc == kcs[-1],
                           [("ring", s_), ("yT", kc)], PK(pp))
                    TT(DVE, XT(dc, tb * 512, 512), XT(dc, tb * 512, 512), ps[pp][:, :], ALU.add, [("xT", dc, tb)],
                       [PK(pp), ("xT", dc, tb)])
                if stop_after is None and (last or l + 1 < depth):
                    if pend_apply:
                        pend_apply.pop(0)()
                    norm_stats(tb, nbn)
                    if last:
                        pend_apply.append((lambda tb_: (lambda: norm_apply(tb_, nbn, lambda c: fin[:, c:c + 1], True)))(tb))
                    else:
                        vb2 = (l + 1) * NV
                        gfn = (lambda vb2_: (lambda c: vec[:, vb2_ + V_NG + c: vb2_ + V_NG + c + 1]))(vb2)
                        pend_apply.append((lambda tb_, g_: (lambda: norm_apply(tb_, nbn, g_, False)))(tb, gfn))
            while pend_apply:
                pend_apply.pop(0)()

            if not last and l + 1 < depth:
                sA[1] = load_ring(win_d[l + 1][:, 512:1024], 512, s2)
                sB2_pre = load_ring(win_d[l + 1][:, 2048:2310], 262, s0)

        if stop_after is not None or depth < DEPTH:
            for c in range(8):
                S.dma(SP, out_d[c * 128:(c + 1) * 128, :], XT(c), reads=[("xT", c, tb) for tb in range(NTB)])
        if dbg:
            for c in range(8):
                S.dma(SP, dbg_y[c * 128:(c + 1) * 128, :], YT(c), reads=[("yT", c)])
        S.finish(SP)
        S.mark('END')
        S.emit()
    nc._marks = S.marks
    return nc


_SPLIT = (256, 256, 256, 256, 384, 384, 256, 256, 6, 256, 128, 32, 384)


def _pack(inputs):
    L = DEPTH
    f = np.float32
    w_in = np.asarray(inputs["w_in"], f)
    offs = np.concatenate([[0], np.cumsum(_SPLIT)])
    seg = {n: (int(offs[i]), int(offs[i + 1])) for i, n in enumerate(
        ["a_h", "a_b", "a_c", "a_z", "s_z", "s_x", "s_b", "s_c", "s_dt", "c_qa", "c_kv", "c_kr", "c_z"])}

    def cols(n, a=None, b=None):
        s0, s1 = seg[n]
        a = 0 if a is None else a
        b = (s1 - s0) if b is None else b
        return np.arange(s0 + a, s0 + b)

    idx = np.concatenate([
        cols("a_h", 0, 128), cols("a_b", 0, 128), cols("a_c", 0, 128), cols("a_z", 0, 128),
        cols("a_h", 128, 256), cols("a_b", 128, 256), cols("a_c", 128, 256), cols("a_z", 128, 256),
        cols("s_x"), cols("s_b"), cols("s_c"), cols("s_z"), cols("s_dt"),
        cols("c_qa"), cols("c_kv"), cols("c_kr"), cols("c_kr", 16, 32), cols("c_kr", 0, 16), cols("c_z")])
    assert idx.shape[0] == N_IN
    w_in_p = np.ascontiguousarray(w_in[:, :, idx])
    w_qb = np.asarray(inputs["w_qb"], f)
    qidx = []
    for h in range(6):
        b0 = h * 96
        qidx += list(range(b0 + 64, b0 + 96)) + list(range(b0 + 80, b0 + 96)) + list(range(b0 + 64, b0 + 80)) + list(range(b0, b0 + 64))
    w_qb_p = np.ascontiguousarray(w_qb[:, :, np.array(qidx)])
    w_kvb = np.asarray(inputs["w_kvb"], f)
    w_kk = np.zeros((L, 128, 768), f)
    w_kv = np.zeros((L, 128, 384), f)
    for h in range(6):
        w_kk[:, :, h * 128 + 64:(h + 1) * 128] = w_kvb[:, :, h * 128:h * 128 + 64]
        w_kv[:, :, h * 64:(h + 1) * 64] = w_kvb[:, :, h * 128 + 64:(h + 1) * 128]
    vecs = np.zeros((L, 128, NV), f)
    colmaj = lambda v: np.asarray(v, f).reshape(-1, 128).T
    for l in range(L):
        vecs[l, :, V_NG:V_NG + 8] = colmaj(inputs["norm_g"][l])
        ca = np.asarray(inputs["conv_a_w"][l], f)
        for j in range(2):
            for k in range(3):
                vecs[l, :, V_CA + j * 3 + k] = ca[k, j * 128:(j + 1) * 128]
        sw = np.asarray(inputs["ssd_conv_w"][l], f)
        for cc in range(7):
            for k in range(4):
                vecs[l, :, V_SCW + cc * 4 + k] = sw[k, cc * 128:(cc + 1) * 128]
        vecs[l, :, V_SCB:V_SCB + 7] = colmaj(inputs["ssd_conv_b"][l])
        vecs[l, :, V_SD:V_SD + 3] = colmaj(np.repeat(np.asarray(inputs["ssd_d"][l], f), 64))
        vecs[l, :, V_SNG:V_SNG + 3] = colmaj(inputs["ssd_norm_g"][l])
        vecs[l, :, V_QNG:V_QNG + 2] = colmaj(inputs["mla_q_norm_g"][l])
        vecs[l, :, V_KNG:V_KNG + 1] = colmaj(inputs["mla_kv_norm_g"][l])
    rowv = np.concatenate([np.asarray(inputs["ssd_dt_bias"], f), np.asarray(inputs["ssd_a_log"], f)], axis=1).reshape(L, 1, 12)
    fin = np.ascontiguousarray(colmaj(inputs["final_norm_g"]))
    k = np.arange(128)
    cfm = np.zeros((128, 515), f)
    cfm[:, 0:128] = np.eye(128, dtype=f)
    cfm[:, 128:256] = (k[:, None] <= k[None, :]).astype(f)
    cfm[:, 256:384] = np.where(k[:, None] <= k[None, :], 0.0, NEG).astype(f)
    cfm[:, 384:512] = 1.0
    inv_freq = (10000.0 ** (-np.arange(0, 32, 2, dtype=np.float32) / np.float32(32))).astype(f)
    p = np.arange(64)
    cfm[0:64, 512] = inv_freq[p % 16]
    ph = np.where(p < 32, 0.5 * math.pi, np.where(p < 48, math.pi, 0.0))
    cfm[0:64, 513] = ph.astype(f)
    cfm[:, 514] = -math.pi
    cbm = np.zeros((128, 1024), f)
    cbm[:, 0:128] = np.eye(128, dtype=f)
    cbm[:, 128:256] = cfm[:, 256:384]
    cbm[:, 256:384] = 1.0
    lo = (k < 64)
    gB = np.repeat(lo[:, None], 128, 1)
    gC = np.repeat(lo[None, :], 128, 0)
    gD = (lo[:, None] == lo[None, :])
    gE = ~gC
    gF = ~gB
    for i, g in enumerate([gB, gC, gD, gE, gF]):
        cbm[:, 384 + i * 128:384 + (i + 1) * 128] = g.astype(f)
    shared = {"w_in": w_in_p, "w_out": np.ascontiguousarray(np.asarray(inputs["w_out"], f)), "w_qb": w_qb_p,
              "w_kk": w_kk, "w_kv": w_kv, "vecs": vecs, "rowv": rowv, "fin_g": fin, "cf": cfm, "cb": cbm}
    x = np.asarray(inputs["x"], f)
    pos = np.asarray(inputs["positions"], np.int32)
    maps = []
    for b in range(8):
        m = dict(shared)
        m["xT"] = np.ascontiguousarray(x[b].T)
        m["pos"] = np.ascontiguousarray(pos[b].reshape(1, S_LEN))
        maps.append(m)
    return maps


def kernel(**inputs):
    maps = _pack(inputs)
    nc = bass.Bass("TRN2", target_bir_lowering=False)
    build(nc)
    res = run_bass_kernel_spmd(nc, maps, core_ids=list(range(8)))
    out = np.stack([np.ascontiguousarray(np.asarray(r["outT"]).T) for r in res.results], axis=0)
    return out.astype(np.float32)
```
